# Optimizing a Trainium2 kernel written in Bass

```python
import math
import jax, jax.numpy as jnp
from jax import lax
import numpy as np

D_MODEL = 4096
BATCH = 2
SEQ = 8192
DEPTH = 1
DEC_BATCH = 8
DEC_SEQ = 2048
PAST_LEN = 128

D_FF = 11008
D_CONV = 2048
CONV_WIDTH = 31
HEAD_DIM = 128
HEADS_PER_GROUP = 8
DILATED_GROUPS = ((128, 1), (512, 4), (2048, 16))
N_GROUPS = len(DILATED_GROUPS)
N_ATTN_HEADS = HEADS_PER_GROUP * N_GROUPS
D_ATTN = N_ATTN_HEADS * HEAD_DIM
D_ATTN_OUT = HEADS_PER_GROUP * HEAD_DIM
N_BRANCHES = 2
C_Q = 2 * D_CONV
C_K = C_Q + D_ATTN
C_V = C_K + D_ATTN
C_GATE = C_V + D_ATTN
D_IN = C_GATE + N_BRANCHES * D_MODEL
D_PLE = 256
NUM_BUCKETS = 32
MAX_DISTANCE = 1024
N_POST_NORMS = 4
LN_EPS = 1e-5
NEG_INF = -1e30
DEEPNORM_ALPHA = (2.0 * DEPTH) ** 0.25
DEEPNORM_BETA = (8.0 * DEPTH) ** -0.25

kernel_name = 'hybrid_conv_dilated_attn_encoder'


def layer_norm(x, g, b):
    xf = x.astype(jnp.float32)
    mu = jnp.mean(xf, axis=-1, keepdims=True)
    var = jnp.mean(jnp.square(xf - mu), axis=-1, keepdims=True)
    return ((xf - mu) * lax.rsqrt(var + LN_EPS) * g + b).astype(x.dtype)


def swiglu(x, w_gate, w_up, w_down):
    return (jax.nn.silu(x @ w_gate) * (x @ w_up)) @ w_down


def rel_bucket(rel):
    half = NUM_BUCKETS // 2
    max_exact = half // 2
    ret = (rel > 0).astype(np.int32) * half
    n = np.abs(rel)
    large = max_exact + (np.log(np.maximum(n, 1) / max_exact) / np.log(MAX_DISTANCE / max_exact)
                         * (half - max_exact)).astype(np.int32)
    large = np.minimum(large, half - 1)
    return (ret + np.where(n < max_exact, n, large)).astype(np.int32)


def conv_module(u_glu, dw, dw_b, g, b, w_o):
    a, gt = jnp.split(u_glu, 2, axis=-1)
    h = a * jax.nn.sigmoid(gt)
    pad = CONV_WIDTH // 2
    h = lax.conv_general_dilated(h, dw[:, None, :], window_strides=(1,), padding=[(pad, pad)],
                                 dimension_numbers=('NWC', 'WIO', 'NWC'),
                                 feature_group_count=D_CONV) + dw_b
    h = jax.nn.silu(layer_norm(h, g, b))
    return h @ w_o


def dilated_window_attention(q, k, v, bias_table, window, dilation):
    B, S, H, E = q.shape
    R = window // 2 // dilation
    QB = R
    L = S // dilation
    nb = -(-L // QB)
    Lp = nb * QB

    def dec(t):
        return t.reshape(B, L, dilation, H, E)

    qd = jnp.pad(dec(q), ((0, 0), (0, Lp - L), (0, 0), (0, 0), (0, 0))).reshape(B, nb, QB, dilation, H, E)

    def kblocks(t):
        tp = jnp.pad(dec(t), ((0, 0), (QB, Lp - L + QB), (0, 0), (0, 0), (0, 0)))
        tp = tp.reshape(B, nb + 2, QB, dilation, H, E)
        return jnp.concatenate([tp[:, :-2], tp[:, 1:-1], tp[:, 2:]], axis=2)

    kb, vb = kblocks(k), kblocks(v)

    qi = np.arange(QB)[:, None]
    kj = np.arange(3 * QB)[None, :]
    rel = kj - QB - qi
    key_pos = np.arange(nb)[:, None, None] * QB + kj[None] - QB
    valid = (np.abs(rel)[None] <= R) & (key_pos >= 0) & (key_pos < L)
    bias = jnp.moveaxis(bias_table[rel_bucket(rel * dilation)], -1, 0).astype(jnp.float32)

    logits = jnp.einsum('bnqrhe,bnkrhe->bnrhqk', qd, kb,
                        preferred_element_type=jnp.float32) * (E ** -0.5)
    logits = jnp.where(valid[None, :, None, None], logits + bias, NEG_INF)
    lse = jax.nn.logsumexp(logits, axis=-1)
    probs = jnp.exp(logits - lse[..., None]).astype(v.dtype)
    o = jnp.einsum('bnrhqk,bnkrhe->bnqrhe', probs, vb)
    o = o.reshape(B, Lp, dilation, H, E)[:, :L].reshape(B, S, H, E)
    lse = jnp.transpose(lse, (0, 1, 4, 2, 3)).reshape(B, Lp, dilation, H)[:, :L].reshape(B, S, H)
    return o, lse


def encoder_layer(x, p_emb, rel_bias, ln_g, ln_b, w_ff1_gate, w_ff1_up, w_ff1_down, w_in, b_in,
                  conv_dw, conv_dw_b, conv_ln_g, conv_ln_b, w_conv_out, w_attn_out, w_out,
                  w_ff2_gate, w_ff2_up, w_ff2_down, w_ple, w_ple_gate, b_ple_gate):
    B, S, _ = x.shape
    x = layer_norm(DEEPNORM_ALPHA * x + 0.5 * swiglu(x, w_ff1_gate, w_ff1_up, w_ff1_down), ln_g[0], ln_b[0])

    u = x @ w_in + b_in
    u_glu, q, k, v, gates = jnp.split(u, [C_Q, C_K, C_V, C_GATE], axis=-1)

    conv_out = conv_module(u_glu, conv_dw, conv_dw_b, conv_ln_g, conv_ln_b, w_conv_out)

    q = q.reshape(B, S, N_ATTN_HEADS, HEAD_DIM)
    k = k.reshape(B, S, N_ATTN_HEADS, HEAD_DIM)
    v = v.reshape(B, S, N_ATTN_HEADS, HEAD_DIM)
    outs, lses = [], []
    for g, (win, dil) in enumerate(DILATED_GROUPS):
        hs = slice(g * HEADS_PER_GROUP, (g + 1) * HEADS_PER_GROUP)
        o, l = dilated_window_attention(q[:, :, hs], k[:, :, hs], v[:, :, hs], rel_bias[:, hs], win, dil)
        outs.append(o)
        lses.append(l)
    wts = jax.nn.softmax(jnp.stack(lses), axis=0).astype(x.dtype)
    attn = jnp.sum(wts[..., None] * jnp.stack(outs), axis=0).reshape(B, S, D_ATTN_OUT)
    attn_out = attn @ w_attn_out

    gate = jax.nn.sigmoid(gates.reshape(B, S, N_BRANCHES, D_MODEL))
    merged = gate[:, :, 0] * conv_out + gate[:, :, 1] * attn_out
    x = layer_norm(DEEPNORM_ALPHA * x + merged @ w_out, ln_g[1], ln_b[1])

    x = layer_norm(DEEPNORM_ALPHA * x + 0.5 * swiglu(x, w_ff2_gate, w_ff2_up, w_ff2_down), ln_g[2], ln_b[2])

    ple = (p_emb @ w_ple) * jax.nn.sigmoid(x @ w_ple_gate + b_ple_gate)
    return layer_norm(DEEPNORM_ALPHA * x + ple, ln_g[3], ln_b[3])


def setup_inputs(seed: int = 0) -> dict:
    key = jax.random.key(seed)
    ks = jax.random.split(key, 32)
    f32 = jnp.float32

    def nrm(k, shape, scale):
        return jax.random.normal(k, shape, f32) * scale

    return {
        'x_prompt': nrm(ks[0], (BATCH, SEQ, D_MODEL), 1.0),
        'x_sample': nrm(ks[1], (DEC_BATCH, DEC_SEQ, D_MODEL), 1.0),
        'p_prompt': nrm(ks[2], (DEPTH, BATCH, SEQ, D_PLE), 1.0),
        'p_sample': nrm(ks[3], (DEPTH, DEC_BATCH, DEC_SEQ, D_PLE), 1.0),
        'rel_bias': nrm(ks[4], (NUM_BUCKETS, N_ATTN_HEADS), 0.5),
        'ln_g': 1.0 + nrm(ks[5], (DEPTH, N_POST_NORMS, D_MODEL), 0.02),
        'ln_b': nrm(ks[6], (DEPTH, N_POST_NORMS, D_MODEL), 0.02),
        'w_ff1_gate': nrm(ks[7], (DEPTH, D_MODEL, D_FF), D_MODEL ** -0.5),
        'w_ff1_up': nrm(ks[8], (DEPTH, D_MODEL, D_FF), D_MODEL ** -0.5),
        'w_ff1_down': nrm(ks[9], (DEPTH, D_FF, D_MODEL), DEEPNORM_BETA * D_FF ** -0.5),
        'w_in': nrm(ks[10], (DEPTH, D_MODEL, D_IN), D_MODEL ** -0.5),
        'b_in': nrm(ks[11], (DEPTH, D_IN), 0.02),
        'conv_dw': nrm(ks[12], (DEPTH, CONV_WIDTH, D_CONV), CONV_WIDTH ** -0.5),
        'conv_dw_b': nrm(ks[13], (DEPTH, D_CONV), 0.02),
        'conv_ln_g': 1.0 + nrm(ks[14], (DEPTH, D_CONV), 0.02),
        'conv_ln_b': nrm(ks[15], (DEPTH, D_CONV), 0.02),
        'w_conv_out': nrm(ks[16], (DEPTH, D_CONV, D_MODEL), DEEPNORM_BETA * D_CONV ** -0.5),
        'w_attn_out': nrm(ks[17], (DEPTH, D_ATTN_OUT, D_MODEL), DEEPNORM_BETA * D_ATTN_OUT ** -0.5),
        'w_out': nrm(ks[18], (DEPTH, D_MODEL, D_MODEL), DEEPNORM_BETA * D_MODEL ** -0.5),
        'w_ff2_gate': nrm(ks[19], (DEPTH, D_MODEL, D_FF), D_MODEL ** -0.5),
        'w_ff2_up': nrm(ks[20], (DEPTH, D_MODEL, D_FF), D_MODEL ** -0.5),
        'w_ff2_down': nrm(ks[21], (DEPTH, D_FF, D_MODEL), DEEPNORM_BETA * D_FF ** -0.5),
        'w_ple': nrm(ks[22], (DEPTH, D_PLE, D_MODEL), DEEPNORM_BETA * D_PLE ** -0.5),
        'w_ple_gate': nrm(ks[23], (DEPTH, D_MODEL, D_MODEL), D_MODEL ** -0.5),
        'b_ple_gate': nrm(ks[24], (DEPTH, D_MODEL), 0.02),
    }


def reference(x_prompt, x_sample, p_prompt, p_sample, rel_bias, ln_g, ln_b, w_ff1_gate, w_ff1_up,
              w_ff1_down, w_in, b_in, conv_dw, conv_dw_b, conv_ln_g, conv_ln_b, w_conv_out,
              w_attn_out, w_out, w_ff2_gate, w_ff2_up, w_ff2_down, w_ple, w_ple_gate, b_ple_gate):
    def run(x, p):
        for i in range(DEPTH):
            x = encoder_layer(x, p[i], rel_bias, ln_g[i], ln_b[i], w_ff1_gate[i], w_ff1_up[i],
                              w_ff1_down[i], w_in[i], b_in[i], conv_dw[i], conv_dw_b[i],
                              conv_ln_g[i], conv_ln_b[i], w_conv_out[i], w_attn_out[i], w_out[i],
                              w_ff2_gate[i], w_ff2_up[i], w_ff2_down[i], w_ple[i], w_ple_gate[i],
                              b_ple_gate[i])
        return x

    y_prompt = run(x_prompt, p_prompt)
    y_sample = run(x_sample, p_sample)
    return (y_prompt, y_sample)
```

```python
import math
from contextlib import ExitStack

import numpy as np
import concourse.bass as bass
import concourse.mybir as mybir
from concourse.bass_utils import run_bass_kernel_spmd

F32 = mybir.dt.float32
BF16 = mybir.dt.bfloat16
AF = mybir.ActivationFunctionType
ALU = mybir.AluOpType

ALPHA = 2.0 ** 0.25
EPS = 1e-5
NEG = -30000.0
T = 512
HALO = 1024
DILS = (1, 4, 16)
CONVW = 31
SCALE = 128 ** -0.5
NSLOT = 3
KR = 8
SAME_ENG_SYNC = False
NTMP = 9


class Cfg:
    def __init__(self, D=4096, DFF=11008, DCONV=2048, HPG=8, DPLE=256, SEQ=8192, QS=2048, HG=44):
        self.D, self.DFF, self.DCONV, self.HPG, self.DPLE, self.SEQ, self.QS, self.HG = D, DFF, DCONV, HPG, DPLE, SEQ, QS, HG
        self.KC = D // 128
        self.NJ = DFF // 128
        self.NCC = DCONV // 128
        self.KP = DPLE // 128
        self.NH = 3 * HPG
        self.DATT = self.NH * 128
        self.C_Q = 2 * DCONV
        self.C_K = self.C_Q + self.DATT
        self.C_V = self.C_K + self.DATT
        self.C_GATE = self.C_V + self.DATT
        self.DIN = self.C_GATE + 2 * D
        self.QP = SEQ // 4
        self.LA = self.QP + 2 * HALO
        self.B0 = self.LA + HALO
        self.NPOS = self.B0 + QS + HALO
        self.t1 = [p for p in range(0, self.LA, T)] + [self.B0 + i * T for i in range(QS // T)]
        self.t2 = [HALO + i * T for i in range(self.QP // T)] + [self.B0 + i * T for i in range(QS // T)]
        o = 0
        self.cv = {}
        for name, n in (("g", 4 * self.KC), ("b", 4 * self.KC), ("bin", self.DIN // 128), ("dw", CONVW * self.NCC),
                        ("dwb", self.NCC), ("cg", self.NCC), ("cb", self.NCC), ("bpg", self.KC)):
            self.cv[name] = o
            o += n
        self.NCV = o
        self.moff = []
        o = 0
        for d in DILS:
            self.moff.append(o)
            nsub = T // d
            o += d * (-(-(nsub + 128) // 128))
        self.NMK = o

    def own(self, p):
        return (HALO <= p < HALO + self.QP) or p >= self.B0

    def glu(self, p):
        return (HALO - T <= p < HALO + self.QP + T) or p >= self.B0


class Prog:
    def __init__(self):
        self.ops = []
        self.lastw = {}
        self.rdc = {}
        self.rdd = {}

    def add(self, eng, fn, reads=(), writes=(), dma=False):
        i = len(self.ops)
        deps = set()
        for r in reads:
            w = self.lastw.get(r)
            if w is not None:
                deps.add(w)
        for r in writes:
            w = self.lastw.get(r)
            if w is not None:
                deps.add(w)
            d = self.rdc.get(r)
            if d:
                deps.update(d.values())
            l = self.rdd.get(r)
            if l:
                deps.update(l)
        self.ops.append((eng, fn, deps, dma))
        for r in reads:
            if dma:
                self.rdd.setdefault(r, []).append(i)
            else:
                self.rdc.setdefault(r, {})[eng] = i
        for r in writes:
            self.lastw[r] = i
            self.rdc[r] = {}
            self.rdd[r] = []
        return i


class _Stop(Exception):
    pass


def build(cfg, limit=99):
    c = cfg

    def stg(n):
        if n > limit:
            raise _Stop()
    D, KC, NJ, NCC, KP, HPG, NH = c.D, c.KC, c.NJ, c.NCC, c.KP, c.HPG, c.NH
    NPOS = c.NPOS
    nc = bass.Bass("TRN2", target_bir_lowering=False)
    P = Prog()

    def din(name, shape, dt=F32):
        return nc.dram_tensor(name, list(shape), dt, kind="ExternalInput").ap()

    def dint(name, shape, dt):
        return nc.dram_tensor(name, list(shape), dt, kind="Internal").ap()

    xa = din("xa", [c.LA, D])
    xb = din("xb", [c.QS, D])
    pa = din("pa", [c.QP, c.DPLE])
    pb = din("pb", [c.QS, c.DPLE])
    wshapes = {
        "w_ff1_gate": (D, c.DFF), "w_ff1_up": (D, c.DFF), "w_ff1_down": (c.DFF, D), "w_in": (D, c.DIN),
        "w_conv_out": (c.DCONV, D), "w_attn_out": (HPG * 128, D), "w_out": (D, D),
        "w_ff2_gate": (D, c.DFF), "w_ff2_up": (D, c.DFF), "w_ff2_down": (c.DFF, D),
        "w_ple": (c.DPLE, D), "w_ple_gate": (D, D),
    }
    wf = {k: din(k, v) for k, v in wshapes.items()}
    wb = {k: dint(k + "_bf", v, BF16) for k, v in wshapes.items()}
    cvec_d = din("cvec", [128, c.NCV])
    biasT_d = din("biasT", [NH, 128, 256])
    kmask_d = din("kmask", [len(c.t2), 128, c.NMK])
    pmask_d = din("pmask", [len(c.t1), 128, T])
    ident_d = din("ident", [128, 128])
    ya = nc.dram_tensor("ya", [c.QP, D], F32, kind="ExternalOutput").ap()
    yb = nc.dram_tensor("yb", [c.QS, D], F32, kind="ExternalOutput").ap()
    x1s = dint("x1s", [D, NPOS], F32)
    x1b = dint("x1b", [D, NPOS], BF16)
    kts = dint("kts", [NH, 128, NPOS], BF16)
    vs = dint("vs", [NH, NPOS + 2048, 128], BF16)
    hg = dint("hg", [c.DCONV, NPOS], F32)

    es = ExitStack()

    def sb(name, shape, dt):
        return es.enter_context(nc.sbuf_tensor(name, list(shape), dt))

    HPAGES = max(c.HG, 2 * (D // 256), KC)
    ybuf = sb("ybuf", [128, 16384], F32)
    xT = sb("xT", [128, KC, T], BF16)
    hbuf = sb("hbuf", [128, HPAGES * 512], BF16)
    wring = [sb("wr%d" % i, [128, 8, 512], BF16) for i in range(NSLOT)]
    tmp = sb("tmp", [128, NTMP, 512], F32)
    cvec = sb("cvec_s", [128, c.NCV], F32)
    cv2 = sb("cv2", [128, 8 * KC + NH], F32)
    ident = sb("ident_s", [128, 128], F32)
    ones32 = sb("ones32", [128, 128], F32)
    onesb = sb("onesb", [128, 128], BF16)
    epsc = sb("epsc", [128, 1], F32)
    bt = sb("bt", [128, 2, 256], F32)
    km = sb("km", [128, 2, c.NMK], F32)
    pmk = sb("pmk", [128, T], F32)
    hw = sb("hw", [128, 2, T + 32], F32)
    pT = sb("pT", [128, KP, T], BF16)
    vst = sb("vst", [128, 4, 512], BF16)
    ps = [es.enter_context(nc.psum_tensor("ps%d" % i, [128, 512], F32)) for i in range(8)]

    yT = ybuf[:, 0:KC * 512].rearrange("p (k t) -> p k t", t=512)

    def YK(lo_kb, hi_kb):
        return [("yp", i) for i in range(int(lo_kb), int(math.ceil(hi_kb)))]

    def yk(kc):
        return [("yp", 2 * kc), ("yp", 2 * kc + 1)]

    def ybf(lo_kb, n_kb):
        return ybuf[:, lo_kb * 256:(lo_kb + n_kb) * 256].bitcast(BF16)

    attnT = ybf(0, HPG).rearrange("p (h t) -> p h t", t=512)
    qT = ybf(8, NH).rearrange("p (h t) -> p h t", t=512)
    kt = [ybf(32, 5), ybf(37, 5)]
    ktk = [YK(32, 37), YK(37, 42)]
    vch = [ybf(42, 8), ybf(50, 8)]
    vchk = [YK(42, 50), YK(50, 58)]
    cvf = ybuf[:, 8 * 256:(8 + 2 * NCC) * 256].rearrange("p (k t) -> p k t", t=512)
    cvs = ybf(40, NCC).rearrange("p (k t) -> p k t", t=512)

    hT = hbuf[:, :].rearrange("p (k t) -> p k t", t=512)

    def hk(j):
        return [("hp", j)]
    SP = D // 256
    stage = [hbuf[:, sbi * SP * 512:(sbi + 1) * SP * 512].bitcast(F32) for sbi in range(2)]
    stagek = [[("hp", sbi * SP + i) for i in range(SP)] for sbi in range(2)]
    mT = hT

    def tk(i):
        return [("t", i)]

    def cvc(name, i=0):
        return cvec[:, c.cv[name] + i:c.cv[name] + i + 1]

    def PSK(b):
        return [("ps", b, q) for q in range(4)]

    nbank = [0]

    def nb():
        b = nbank[0] % 8
        nbank[0] += 1
        return b

    def pe(fn, r, w):
        return P.add("pe", fn, r, w)

    def act(fn, r, w):
        return P.add("act", fn, r, w)

    def dve(fn, r, w):
        return P.add("dve", fn, r, w)

    def dma(q, out, in_, r, w):
        return P.add(q, lambda e: e.dma_start(out=out, in_=in_), r, w, dma=True)

    CK = [("c", "consts")]

    dma("act", cvec[:, :], cvec_d[:, :], [], [("c", "cvec")])
    dma("act", ident[:, :], ident_d[:, :], [], [("c", "ident")])
    dve(lambda e: e.memset(ones32[:, :], 1.0), [], [("c", "ones32")])
    dve(lambda e: e.memset(onesb[:, :], 1.0), [], [("c", "onesb")])
    dve(lambda e: e.memset(epsc[:, :], EPS), [], [("c", "eps")])
    for i in range(4):
        s_ = ALPHA if i < 3 else 1.0
        dve(lambda e, i=i, s_=s_: e.tensor_scalar(cv2[:, i * KC:(i + 1) * KC], cvec[:, c.cv["g"] + i * KC:c.cv["g"] + (i + 1) * KC], s_, None, ALU.mult),
            [("c", "cvec")], [("c", "cv2", i)])
        dve(lambda e, i=i, s_=s_: e.tensor_scalar(cv2[:, (4 + i) * KC:(5 + i) * KC], cvec[:, c.cv["b"] + i * KC:c.cv["b"] + (i + 1) * KC], s_, None, ALU.mult),
            [("c", "cvec")], [("c", "cv2", 4 + i)])
    qc0 = c.cv["bin"] + c.C_Q // 128
    dve(lambda e: e.tensor_scalar(cv2[:, 8 * KC:8 * KC + NH], cvec[:, qc0:qc0 + NH], SCALE, None, ALU.mult), [("c", "cvec")], [("c", "bqs")])
    CALL = [("c", "cvec"), ("c", "bqs")] + [("c", "cv2", i) for i in range(8)]

    zb = tmp[:, 0, :].bitcast(BF16)
    dve(lambda e: e.memset(tmp[:, 0, :], 0.0), [], tk(0))
    gaps = [c.LA, c.B0 + c.QS]
    for g0_ in (gaps if limit >= -1 else []):
        tl = [g0_ // T, g0_ // T + 1]
        for h in range(NH):
            dma("pool", kts[h, :, g0_:g0_ + HALO], zb[:, 0:HALO], tk(0), [("kt", t_, h) for t_ in tl])
            dma("pool", vs[h, g0_:g0_ + HALO, :].rearrange("(p a) e -> p (a e)", p=128), zb[:, 0:HALO], tk(0), [("vs", t_, h, m_) for t_ in tl for m_ in range(4)])
    for hp_, tix in (((c.B0 - 16, (c.B0 - 16) // T), (c.B0 + c.QS, (c.B0 + c.QS) // T)) if limit >= -1 else []):
        dma("pool", hg[:, hp_:hp_ + 16].rearrange("(k p) t -> p k t", p=128),
            tmp[:, 0, 0:NCC * 16].rearrange("p (k t) -> p k t", t=16), tk(0), [("hg", tix, cg_) for cg_ in range((NCC + 3) // 4)])

    wblk = {}
    order = ["w_ff1_gate", "w_ff1_up", "w_ff1_down", "w_in", "w_conv_out", "w_attn_out", "w_out",
             "w_ff2_gate", "w_ff2_up", "w_ff2_down", "w_ple_gate", "w_ple"]
    import os as _os
    _wonly = _os.environ.get("WONLY")
    for name in ((order if not _wonly else _wonly.split(",")) if limit >= 0 else []):
        Kr, N = wshapes[name]
        R = max(16, ((2 * 1024 * 1024) // N) // 16 * 16)
        wblk[name] = R
        for bi, r0 in enumerate(range(0, Kr, R)):
            r1 = min(Kr, r0 + R)
            dma("pool", wb[name][r0:r1, :], wf[name][r0:r1, :], [], [("wb", name, bi)])

    def wbkeys(name, kc0, R):
        Rb = wblk[name]
        return [("wb", name, bi) for bi in range((kc0 * 128) // Rb, ((kc0 + R) * 128 - 1) // Rb + 1)]

    wn = [0]

    def wtile(name, kc0, R, c0, W):
        s = wn[0] % NSLOT
        wn[0] += 1
        src = wb[name].rearrange("(k p) n -> p k n", p=128)[:, kc0:kc0 + R, c0:c0 + W]
        dma("sp", wring[s][:, 0:R, 0:W], src, wbkeys(name, kc0, R), [("w", s)])
        return s

    def mm_fm(name, k0, k1, col0, W, rhs, rkeys, banks, N=T):
        ncc = W // 128
        for kc0 in range(k0, k1, 8):
            R = min(8, k1 - kc0)
            s = wtile(name, kc0, R, col0, W)

            def fn(e, s=s, kc0=kc0, R=R):
                ins = None
                for r in range(R):
                    for cc in range(ncc):
                        ins = e.matmul(ps[banks[cc]][:, 0:N], lhsT=wring[s][:, r, cc * 128:(cc + 1) * 128], rhs=rhs(kc0 + r),
                                       start=(kc0 + r == k0), stop=(kc0 + r == k1 - 1))
                return ins
            reads = [("w", s)]
            for r in range(R):
                reads += rkeys(kc0 + r)
            writes = []
            for b in banks[:ncc]:
                writes += PSK(b)
            import os as _os2
            if not ("M0" in _os2.environ.get("SKIP", "") and name == "w_in" and col0 >= c.C_Q and col0 < c.C_K):
                pe(fn, reads, writes)

    def proj_fm(name, k0, k1, col0, ncols, rhs, rkeys, evac):
        for g0 in range(0, ncols, 512):
            W = min(512, ncols - g0)
            banks = [nb() for _ in range(W // 128)]
            mm_fm(name, k0, k1, col0 + g0, W, rhs, rkeys, banks)
            for cc in range(W // 128):
                evac(g0 // 128 + cc, banks[cc])

    tset = [0]

    def gated(n1, k1, col1, rhs1, rk1, func, bias1, n2, k2, col2, rhs2, rk2, ncols, fin, post=None):
        for g0 in range(0, ncols, 512):
            W = min(512, ncols - g0)
            ncc = W // 128
            ts = [(tset[0] % 2) * 4 + i for i in range(4)]
            tset[0] += 1
            b1 = [nb() for _ in range(ncc)]
            mm_fm(n1, 0, k1, col1 + g0, W, rhs1, rk1, b1)
            for cc in range(ncc):
                ci = g0 // 128 + cc
                bia = bias1(ci) if bias1 is not None else 0.0
                act(lambda e, t_=ts[cc], b_=b1[cc], bia=bia: e.activation(out=tmp[:, t_, :], in_=ps[b_][:, :], func=func, bias=bia),
                    PSK(b1[cc]) + CALL, tk(ts[cc]))
                if post is not None:
                    post(ts[cc])
            b2 = [nb() for _ in range(ncc)]
            mm_fm(n2, 0, k2, col2 + g0, W, rhs2, rk2, b2)
            for cc in range(ncc):
                fin(g0 // 128 + cc, cc, b2[cc], ts[cc])
            yield g0

    xrhs = (lambda kc: xT[:, kc, :])
    xrk = (lambda kc: [("xT", kc)])

    def ffn(wg, wu, wd):
        for j0 in range(0, NJ, c.HG):
            j1 = min(NJ, j0 + c.HG)

            def fin(ci, cc, b2, tslot, j0=j0):
                dve(lambda e: e.tensor_tensor(out=hT[:, ci, :], in0=ps[b2][:, :], in1=tmp[:, tslot, :], op=ALU.mult),
                    PSK(b2) + tk(tslot), hk(ci))
            for _ in gated(wg, KC, j0 * 128, xrhs, xrk, AF.Silu, None, wu, KC, j0 * 128, xrhs, xrk, (j1 - j0) * 128, fin):
                pass

            def evac(m, b):
                dve(lambda e: e.scalar_tensor_tensor(out=yT[:, m, :], in0=ps[b][:, :], scalar=0.5, in1=yT[:, m, :], op0=ALU.mult, op1=ALU.add),
                    PSK(b) + yk(m), yk(m))
            proj_fm(wd, j0, j1, 0, D, (lambda j, j0=j0: hT[:, j - j0, :]), (lambda j, j0=j0: hk(j - j0)), evac)

    MEAN, RSTD, M2 = 0, 1, 2

    def layer_norm(z, zkeys, NCH, Dn, gcol, bcol, sgcol, sbcol, y_out, xo, func):
        bs_, bq_ = nb(), nb()
        for kc in range(NCH):
            i = 3 + kc % 2
            act(lambda e, kc=kc, i=i: e.activation(out=tmp[:, i, :], in_=z(kc), func=AF.Square), zkeys(kc), tk(i))
            pe(lambda e, kc=kc, i=i: (e.matmul(ps[bs_][:, :], lhsT=ones32[:, :], rhs=z(kc), start=(kc == 0), stop=(kc == NCH - 1)),
                                      e.matmul(ps[bq_][:, :], lhsT=ones32[:, :], rhs=tmp[:, i, :], start=(kc == 0), stop=(kc == NCH - 1)))[-1],
               zkeys(kc) + tk(i) + [("c", "ones32")], PSK(bs_) + PSK(bq_))
        act(lambda e: e.activation(out=tmp[:, MEAN, :], in_=ps[bs_][:, :], func=AF.Identity, scale=1.0 / Dn), PSK(bs_), tk(MEAN))
        dve(lambda e: e.tensor_tensor(out=tmp[:, M2, :], in0=tmp[:, MEAN, :], in1=tmp[:, MEAN, :], op=ALU.mult), tk(MEAN), tk(M2))
        dve(lambda e: e.scalar_tensor_tensor(out=tmp[:, RSTD, :], in0=ps[bq_][:, :], scalar=1.0 / Dn, in1=tmp[:, M2, :], op0=ALU.mult, op1=ALU.subtract),
            PSK(bq_) + tk(M2), tk(RSTD))
        act(lambda e: e.activation(out=tmp[:, M2, :], in_=tmp[:, RSTD, :], func=AF.Sqrt, bias=epsc[:, 0:1]), tk(RSTD) + [("c", "eps")], tk(M2))
        dve(lambda e: e.reciprocal(out=tmp[:, RSTD, :], in_=tmp[:, M2, :]), tk(M2), tk(RSTD))
        for kc in range(NCH):
            i1 = 5 + kc % 2
            i2 = 7 + kc % 2
            dve(lambda e, kc=kc, i1=i1: e.tensor_tensor(out=tmp[:, i1, :], in0=z(kc), in1=tmp[:, MEAN, :], op=ALU.subtract), zkeys(kc) + tk(MEAN), tk(i1))
            dve(lambda e, i1=i1, i2=i2: e.tensor_tensor(out=tmp[:, i2, :], in0=tmp[:, i1, :], in1=tmp[:, RSTD, :], op=ALU.mult), tk(i1) + tk(RSTD), tk(i2))
            if xo is not None:
                o_ap, o_k = xo(kc)
                act(lambda e, kc=kc, i2=i2, o_ap=o_ap: e.activation(out=o_ap, in_=tmp[:, i2, :], func=func, bias=bcol(kc), scale=gcol(kc)), tk(i2) + CALL, o_k)
            if y_out is not None:
                o_ap, o_k = y_out(kc)
                act(lambda e, kc=kc, i2=i2, o_ap=o_ap: e.activation(out=o_ap, in_=tmp[:, i2, :], func=AF.Identity, bias=sbcol(kc), scale=sgcol(kc)), tk(i2) + CALL, o_k)

    def main_ln(idx, want_x=True):
        g0 = c.cv["g"] + idx * KC
        b0 = c.cv["b"] + idx * KC
        layer_norm(lambda kc: yT[:, kc, :], yk, KC, D,
                   lambda kc: cvec[:, g0 + kc:g0 + kc + 1], lambda kc: cvec[:, b0 + kc:b0 + kc + 1],
                   lambda kc: cv2[:, idx * KC + kc:idx * KC + kc + 1], lambda kc: cv2[:, (4 + idx) * KC + kc:(4 + idx) * KC + kc + 1],
                   lambda kc: (yT[:, kc, :], yk(kc)),
                   (lambda kc: (xT[:, kc, :], [("xT", kc)])) if want_x else None, AF.Identity)

    GK = min(4, KC)

    def body():
        stg(1)
        for ti, pos0 in enumerate(c.t1):
            inA = pos0 < c.LA
            tix = pos0 // T
            import os as _os
            _sk = _os.environ.get("SKIP", "")
            if "P" not in _sk:
                dma("act", pmk[:, :], pmask_d[ti], [], [("pmk",)])
            for m in range(4):
                sbi = m % 2
                src = xa[pos0 + m * 128:pos0 + (m + 1) * 128, :] if inA else xb[pos0 - c.B0 + m * 128:pos0 - c.B0 + (m + 1) * 128, :]
                dma("act", stage[sbi][:, :], src, [], stagek[sbi])
                for kq in range(KC // GK):
                    b = nb()
                    pe(lambda e, b=b, kq=kq, sbi=sbi: [e.transpose(out=ps[b][:, i * 128:(i + 1) * 128], in_=stage[sbi][:, (kq * GK + i) * 128:(kq * GK + i + 1) * 128], identity=ident[:, :])
                                                        for i in range(GK)][-1], stagek[sbi] + [("c", "ident")], PSK(b))
                    yks = []
                    xks = []
                    for i in range(GK):
                        yks += yk(kq * GK + i)
                        xks += [("xT", kq * GK + i)]
                    pv = (lambda b: ps[b][:, 0:GK * 128].rearrange("p (i t) -> p i t", i=GK))
                    if "A" not in _sk:
                        act(lambda e, b=b, kq=kq, m=m: e.activation(out=yT[:, kq * GK:(kq + 1) * GK, m * 128:(m + 1) * 128], in_=pv(b), func=AF.Identity, scale=ALPHA), PSK(b), yks)
                    if "D" not in _sk:
                        dve(lambda e, b=b, kq=kq, m=m: e.tensor_copy(out=xT[:, kq * GK:(kq + 1) * GK, m * 128:(m + 1) * 128], in_=pv(b)), PSK(b) + yks, xks)
            stg(2)
            ffn("w_ff1_gate", "w_ff1_up", "w_ff1_down")
            stg(3)
            main_ln(0)
            stg(4)
            if c.own(pos0):
                for k0 in range(0, KC, 8):
                    k1 = min(KC, k0 + 8)
                    ks = []
                    xs = []
                    for kc in range(k0, k1):
                        ks += yk(kc)
                        xs += [("xT", kc)]
                    dma("pool", x1s[k0 * 128:k1 * 128, pos0:pos0 + T].rearrange("(k p) t -> p k t", p=128), yT[:, k0:k1, :], ks, [("x1s", pos0, k0)])
                    dma("pool", x1b[k0 * 128:k1 * 128, pos0:pos0 + T].rearrange("(k p) t -> p k t", p=128), xT[:, k0:k1, :], xs, [("x1b", pos0, k0)])
            stg(5)
            kb0 = c.cv["bin"] + c.C_K // 128
            for g0 in range(0, c.DATT, 512):
                W = min(512, c.DATT - g0)
                nh_ = W // 128
                banks = [nb() for _ in range(nh_)]
                mm_fm("w_in", 0, KC, c.C_K + g0, W, xrhs, xrk, banks)
                h0 = g0 // 128
                for cc in range(nh_):
                    act(lambda e, cc=cc, b=banks[cc], h=h0 + cc: e.activation(out=vst[:, cc, :], in_=ps[b][:, :], func=AF.Identity, bias=cvec[:, kb0 + h:kb0 + h + 1]),
                        PSK(banks[cc]) + CALL, [("vst", cc)])
                dma("pool", kts[h0:h0 + nh_, :, pos0:pos0 + T].rearrange("h p t -> p h t"), vst[:, 0:nh_, :], [("vst", i) for i in range(nh_)],
                    [("kt", tix, h0 + i) for i in range(nh_)])
            stg(6)
            for g0 in range(0, c.DATT, 512):
                W = min(512, c.DATT - g0)
                nh_ = W // 128
                banks = [nb() for _ in range(4)]
                for kc0 in range(0, KC, 8):
                    R = min(8, KC - kc0)
                    s = wtile("w_in", kc0, R, c.C_V + g0, W)

                    def fn(e, s=s, kc0=kc0, R=R, banks=banks, W=W):
                        ins = None
                        for r in range(R):
                            for m in range(4):
                                ins = e.matmul(ps[banks[m]][:, 0:W], lhsT=xT[:, kc0 + r, m * 128:(m + 1) * 128], rhs=wring[s][:, r, 0:W],
                                               start=(kc0 + r == 0), stop=(kc0 + r == KC - 1))
                        return ins
                    rd = [("w", s)] + [("xT", kc0 + r) for r in range(R)]
                    wr = []
                    for b in banks:
                        wr += PSK(b)
                    pe(fn, rd, wr)
                for m in range(4):
                    if m % 2 == 0:
                        dve(lambda e, m=m, b=banks[m], W=W: e.tensor_copy(out=vst[:, m, 0:W], in_=ps[b][:, 0:W]), PSK(banks[m]), [("vst", m)])
                    else:
                        act(lambda e, m=m, b=banks[m], W=W: e.activation(out=vst[:, m, 0:W], in_=ps[b][:, 0:W], func=AF.Identity), PSK(banks[m]), [("vst", m)])
                h0 = g0 // 128
                for m in range(4):
                    dma("pool", vs[h0:h0 + nh_, pos0 + m * 128:pos0 + (m + 1) * 128, :].rearrange("h p e -> p h e"),
                        vst[:, m, 0:W].rearrange("p (h e) -> p h e", e=128), [("vst", m)], [("vs", tix, h0 + i, m) for i in range(nh_)])
            stg(7)
            if c.glu(pos0):
                gb0 = c.cv["bin"]

                def post(tslot):
                    dve(lambda e: e.tensor_tensor(out=tmp[:, tslot, :], in0=tmp[:, tslot, :], in1=pmk[:, :], op=ALU.mult), tk(tslot) + [("pmk",)], tk(tslot))

                def fin(ci, cc, b2, tslot):
                    oslot = (tslot + 4) % 8
                    dve(lambda e: e.scalar_tensor_tensor(out=tmp[:, oslot, :], in0=ps[b2][:, :], scalar=cvec[:, gb0 + ci:gb0 + ci + 1], in1=tmp[:, tslot, :], op0=ALU.add, op1=ALU.mult),
                        PSK(b2) + tk(tslot) + CALL, tk(oslot))
                for g0 in gated("w_in", KC, c.DCONV, xrhs, xrk, AF.Sigmoid, (lambda ci: cvec[:, gb0 + NCC + ci:gb0 + NCC + ci + 1]),
                                "w_in", KC, 0, xrhs, xrk, c.DCONV, fin, post):
                    ncc = min(512, c.DCONV - g0) // 128
                    ci0 = g0 // 128
                    ob = (((tset[0] - 1) % 2) * 4 + 4) % 8
                    dma("pool", hg[ci0 * 128:(ci0 + ncc) * 128, pos0:pos0 + T].rearrange("(k p) t -> p k t", p=128), tmp[:, ob:ob + ncc, :],
                        [("t", ob + i) for i in range(ncc)], [("hg", tix, g0 // 512)])
                    tset[0] += 1

        stg(8)
        for ti, pos0 in enumerate(c.t2):
            inA = pos0 < c.LA
            tix = pos0 // T
            for k0 in range(0, KC, 8):
                k1 = min(KC, k0 + 8)
                dma("pool", xT[:, k0:k1, :], x1b[k0 * 128:k1 * 128, pos0:pos0 + T].rearrange("(k p) t -> p k t", p=128),
                    [("x1b", pos0, k0)], [("xT", kc) for kc in range(k0, k1)])
            kmb = ti % 2
            dma("pool", km[:, kmb, :], kmask_d[ti], [], [("km", kmb)])
            stg(9)
            qb0 = c.cv["bin"] + c.C_Q // 128

            def qevac(ci, b):
                d = DILS[ci // HPG]
                o_ = qT[:, ci, :] if d == 1 else qT[:, ci, :].rearrange("p (r j) -> p j r", r=d)
                i_ = ps[b][:, :] if d == 1 else ps[b][:, :].rearrange("p (j r) -> p j r", r=d)
                if "Q1" in _sk:
                    o_, i_ = qT[:, ci, :], ps[b][:, :]
                if "Q2" in _sk:
                    o_, i_ = vst[:, ci % 4, :], ps[b][:, :]
                if "Q4" in _sk:
                    return
                if "Q3" in _sk:
                    act(lambda e: e.activation(out=o_, in_=i_, func=AF.Identity, bias=cvec[:, 0:1]), PSK(b) + CALL, YK(8 + ci, 9 + ci))
                else:
                    act(lambda e: e.activation(out=o_, in_=i_, func=AF.Identity, bias=cv2[:, 8 * KC + ci:8 * KC + ci + 1], scale=SCALE), PSK(b) + CALL, YK(8 + ci, 9 + ci))
            proj_fm("w_in", 0, KC, c.C_Q, c.DATT, xrhs, xrk, qevac)
            stg(10)
            akeys = [("tS", i_) for i_ in range(8)] + [("pT", i_) for i_ in range(8)]
            dve(lambda e: e.memset(tmp[0:1, 0, 0:1], 0.0), [], tk(0) + tk(1) + tk(2) + akeys)
            nload = [0]
            for hi in range(HPG):
                bU = [0, 1, 2]
                bZ = [3, 4, 5]
                for g, d in enumerate(DILS):
                    head = g * HPG + hi
                    nsub = T // d
                    nq = min(128, nsub)
                    nqb = nsub // nq
                    WLs = nsub + 128
                    nwc = -(-WLs // 128)
                    lb = nload[0] % 2
                    nload[0] += 1
                    w0 = pos0 - 64 * d
                    WL = T + 128 * d
                    tl = list(range(w0 // T, (w0 + WL - 1) // T + 1))
                    dma("pool", kt[lb][:, 0:WL], kts[head, :, w0:w0 + WL], [("kt", t_, head) for t_ in tl], ktk[lb])
                    vv = vch[lb][:, 0:nwc * d * 128].rearrange("p (w r e) -> p w r e", r=d, e=128)
                    dma("pool", vch[lb][:, 0:nwc * d * 128].rearrange("p (w x) -> p w x", w=nwc),
                        vs[head, w0:w0 + nwc * 128 * d, :].rearrange("(w i r) e -> i w (r e)", w=nwc, r=d),
                        [("vs", t_, head, m_) for t_ in tl for m_ in range(4)], vchk[lb])
                    vkeys = vchk[lb]
                    dma("pool", bt[:, lb, :], biasT_d[head], [], [("bt", lb)])
                    ktv = kt[lb][:, 0:WL].rearrange("p (a r) -> p a r", r=d)
                    pend = []
                    blk = 0
                    for r in range(d):
                        for qb in range(nqb):
                            col0 = r * nsub + qb * nq
                            cur = []
                            for cch in range(2):
                                wc = (qb * nq) // 128 + cch
                                nk = min(128, nq + 128 - cch * 128)
                                sq = (blk * 2 + cch) % 8
                                sbk, sqq = 6 + cch, sq % 4
                                S = ps[sbk][0:nk, 0:nq]
                                tS = tmp[0:nk, 0, sqq * 128:sqq * 128 + nq] if sq < 4 else tmp[0:nk, 1, sqq * 128:sqq * 128 + nq]
                                tSk = ("tS", sq)
                                pTt = tmp[:, 2, :].bitcast(BF16)[0:nk, sq * 128:sq * 128 + nq]
                                pk = ("pT", sq)
                                pe(lambda e, S=S, wc=wc, nk=nk, r=r, col0=col0, nq=nq, ktv=ktv, hh=head: e.matmul(S, lhsT=ktv[:, wc * 128:wc * 128 + nk, r], rhs=qT[:, hh, col0:col0 + nq], start=True, stop=True),
                                   ktk[lb] + YK(8 + head, 9 + head), PSK(sbk))
                                dve(lambda e, S=S, tS=tS, nk=nk, cch=cch, nq=nq, lb=lb: e.tensor_tensor(out=tS, in0=S, in1=bt[0:nk, lb, cch * 128:cch * 128 + nq], op=ALU.add),
                                    PSK(sbk) + [("bt", lb)], [tSk])
                                mcol = c.moff[g] + wc * d + r
                                act(lambda e, tS=tS, pTt=pTt, nk=nk, mcol=mcol, kmb=kmb: e.activation(out=pTt, in_=tS, func=AF.Exp, bias=km[0:nk, kmb, mcol:mcol + 1]),
                                    [tSk, ("km", kmb)], [pk])
                                cur.append((pTt, pk, nk, wc, cch, col0, nq, r))
                            for (pTt, pk, nk, wc, cch, col0_, nq_, r_) in pend:
                                pe(lambda e, pTt=pTt, nk=nk, wc=wc, cch=cch, col0_=col0_, nq_=nq_, r_=r_, vv=vv, g=g: (
                                    e.matmul(ps[bU[g]][:, col0_:col0_ + nq_], lhsT=vv[0:nk, wc, r_, :], rhs=pTt, start=(cch == 0), stop=(cch == 1)),
                                    e.matmul(ps[bZ[g]][:, col0_:col0_ + nq_], lhsT=onesb[0:nk, :], rhs=pTt, start=(cch == 0), stop=(cch == 1)))[-1],
                                   [pk, ("c", "onesb")] + vkeys, PSK(bU[g]) + PSK(bZ[g]))
                            pend = cur
                            blk += 1
                    for (pTt, pk, nk, wc, cch, col0_, nq_, r_) in pend:
                        pe(lambda e, pTt=pTt, nk=nk, wc=wc, cch=cch, col0_=col0_, nq_=nq_, r_=r_, vv=vv, g=g: (
                            e.matmul(ps[bU[g]][:, col0_:col0_ + nq_], lhsT=vv[0:nk, wc, r_, :], rhs=pTt, start=(cch == 0), stop=(cch == 1)),
                            e.matmul(ps[bZ[g]][:, col0_:col0_ + nq_], lhsT=onesb[0:nk, :], rhs=pTt, start=(cch == 0), stop=(cch == 1)))[-1],
                           [pk, ("c", "onesb")] + vkeys, PSK(bU[g]) + PSK(bZ[g]))
                vb0 = c.cv["bin"] + c.C_V // 128
                for g, d in enumerate(DILS):
                    head = g * HPG + hi
                    zs, aa = 3 + g, 6 + g
                    nat = (lambda ap, d=d: ap if d == 1 else ap.rearrange("p (j r) -> p j r", r=d))
                    prm = (lambda ap, d=d: ap if d == 1 else ap.rearrange("p (r j) -> p j r", r=d))
                    act(lambda e, g=g, zs=zs, nat=nat, prm=prm: e.activation(out=nat(tmp[:, zs, :]), in_=prm(ps[bZ[g]][:, :]), func=AF.Identity),
                        PSK(bZ[g]), tk(zs))
                    dve(lambda e, g=g, zs=zs, aa=aa, head=head, nat=nat, prm=prm: e.scalar_tensor_tensor(out=nat(tmp[:, aa, :]), in0=nat(tmp[:, zs, :]),
                                                                                          scalar=cvec[:, vb0 + head:vb0 + head + 1], in1=prm(ps[bU[g]][:, :]), op0=ALU.mult, op1=ALU.add),
                        tk(zs) + PSK(bU[g]) + CALL, tk(aa))
                dve(lambda e: e.tensor_tensor(out=tmp[:, 6, :], in0=tmp[:, 6, :], in1=tmp[:, 7, :], op=ALU.add), tk(6) + tk(7), tk(6))
                dve(lambda e: e.tensor_tensor(out=tmp[:, 6, :], in0=tmp[:, 6, :], in1=tmp[:, 8, :], op=ALU.add), tk(6) + tk(8), tk(6))
                dve(lambda e: e.tensor_tensor(out=tmp[:, 3, :], in0=tmp[:, 3, :], in1=tmp[:, 4, :], op=ALU.add), tk(3) + tk(4), tk(3))
                dve(lambda e: e.tensor_tensor(out=tmp[:, 3, :], in0=tmp[:, 3, :], in1=tmp[:, 5, :], op=ALU.add), tk(3) + tk(5), tk(3))
                dve(lambda e: e.reciprocal(out=tmp[:, 4, :], in_=tmp[:, 3, :]), tk(3), tk(4))
                dve(lambda e, hi=hi: e.tensor_tensor(out=attnT[:, hi, :], in0=tmp[:, 6, :], in1=tmp[:, 4, :], op=ALU.mult), tk(6) + tk(4), YK(hi, hi + 1))
            dve(lambda e: e.memset(tmp[0:1, 0, 0:1], 0.0), [], tk(0) + tk(1) + tk(2) + akeys)
            stg(11)
            dw0 = c.cv["dw"]
            for ch in range(NCC):
                hb = ch % 2
                lo = pos0 - 15
                tl = sorted(set([lo // T, pos0 // T, (pos0 + T + 14) // T]))
                dma("pool", hw[:, hb, 0:T + 30], hg[ch * 128:(ch + 1) * 128, lo:lo + T + 30], [("hg", t_, ch // 4) for t_ in tl], [("hw", hb)])
                ck = YK(8 + 2 * ch, 10 + 2 * ch)
                dve(lambda e, ch=ch, hb=hb: e.tensor_scalar(cvf[:, ch, :], hw[:, hb, 0:T], cvec[:, dw0 + ch:dw0 + ch + 1], cvc("dwb", ch), ALU.mult, ALU.add),
                    [("hw", hb)] + CALL, ck)
                for w in range(1, CONVW):
                    dve(lambda e, ch=ch, hb=hb, w=w: e.scalar_tensor_tensor(out=cvf[:, ch, :], in0=hw[:, hb, w:w + T], scalar=cvec[:, dw0 + w * NCC + ch:dw0 + w * NCC + ch + 1],
                                                                            in1=cvf[:, ch, :], op0=ALU.mult, op1=ALU.add), [("hw", hb)] + ck + CALL, ck)
            layer_norm(lambda kc: cvf[:, kc, :], lambda kc: YK(8 + 2 * kc, 10 + 2 * kc), NCC, c.DCONV,
                       lambda kc: cvc("cg", kc), lambda kc: cvc("cb", kc), None, None, None,
                       lambda kc: (cvs[:, kc, :], YK(40 + kc, 41 + kc)), AF.Silu)
            stg(12)
            gt0 = c.cv["bin"] + c.C_GATE // 128
            cvrhs = (lambda kc: cvs[:, kc, :])
            cvrk = (lambda kc: YK(40 + kc, 41 + kc))
            arhs = (lambda kc: attnT[:, kc, :])
            ark = (lambda kc: YK(kc, kc + 1))

            def fin_a(ci, cc, b2, tslot):
                dve(lambda e: e.tensor_tensor(out=tmp[:, tslot, :], in0=ps[b2][:, :], in1=tmp[:, tslot, :], op=ALU.mult), PSK(b2) + tk(tslot), tk(tslot))
            for cg in range(0, D, 512):
                W = min(512, D - cg)
                tset[0] = 0
                for _ in gated("w_in", KC, c.C_GATE + cg, xrhs, xrk, AF.Sigmoid, (lambda ci, cg=cg: cvec[:, gt0 + cg // 128 + ci:gt0 + cg // 128 + ci + 1]),
                               "w_conv_out", NCC, cg, cvrhs, cvrk, W, fin_a):
                    pass

                def fin_b(ci, cc, b2, tslot, cg=cg):
                    m = cg // 128 + ci
                    dve(lambda e: e.tensor_tensor(out=tmp[:, tslot, :], in0=ps[b2][:, :], in1=tmp[:, tslot, :], op=ALU.mult), PSK(b2) + tk(tslot), tk(tslot))
                    dve(lambda e: e.tensor_tensor(out=mT[:, m, :], in0=tmp[:, tslot, :], in1=tmp[:, tslot - 4, :], op=ALU.add), tk(tslot) + tk(tslot - 4), hk(m))
                for _ in gated("w_in", KC, c.C_GATE + D + cg, xrhs, xrk, AF.Sigmoid, (lambda ci, cg=cg: cvec[:, gt0 + KC + cg // 128 + ci:gt0 + KC + cg // 128 + ci + 1]),
                               "w_attn_out", HPG, cg, arhs, ark, W, fin_b):
                    pass
            stg(13)
            for k0 in range(0, KC, 8):
                k1 = min(KC, k0 + 8)
                ks = []
                for kc in range(k0, k1):
                    ks += yk(kc)
                dma("pool", yT[:, k0:k1, :], x1s[k0 * 128:k1 * 128, pos0:pos0 + T].rearrange("(k p) t -> p k t", p=128), [("x1s", pos0, k0)], ks)

            def evac_o(m, b):
                dve(lambda e: e.tensor_tensor(out=yT[:, m, :], in0=ps[b][:, :], in1=yT[:, m, :], op=ALU.add), PSK(b) + yk(m), yk(m))
            proj_fm("w_out", 0, KC, 0, D, (lambda kc: mT[:, kc, :]), hk, evac_o)
            stg(14)
            main_ln(1)
            stg(15)
            ffn("w_ff2_gate", "w_ff2_up", "w_ff2_down")
            main_ln(2)
            stg(16)
            psrc = pa if inA else pb
            prow = pos0 - HALO if inA else pos0 - c.B0
            pst = tmp[:, 7:9, :].rearrange("p a t -> p (a t)")[:, 0:4 * c.DPLE].rearrange("p (m f) -> p m f", m=4)
            dma("act", pst, psrc[prow:prow + T, :].rearrange("(m p) f -> p m f", p=128), [], tk(7) + tk(8))
            for kp in range(KP):
                b = nb()
                pe(lambda e, b=b, kp=kp: [e.transpose(out=ps[b][:, m * 128:(m + 1) * 128], in_=pst[:, m, kp * 128:(kp + 1) * 128], identity=ident[:, :]) for m in range(4)][-1],
                   tk(7) + tk(8) + [("c", "ident")], PSK(b))
                dve(lambda e, b=b, kp=kp: e.tensor_copy(out=pT[:, kp, :], in_=ps[b][:, :]), PSK(b), [("pT_", kp)])
            bp0 = c.cv["bpg"]

            def fin_p(ci, cc, b2, tslot):
                dve(lambda e: e.tensor_tensor(out=tmp[:, tslot, :], in0=ps[b2][:, :], in1=tmp[:, tslot, :], op=ALU.mult), PSK(b2) + tk(tslot), tk(tslot))
                dve(lambda e: e.tensor_tensor(out=yT[:, ci, :], in0=yT[:, ci, :], in1=tmp[:, tslot, :], op=ALU.add), yk(ci) + tk(tslot), yk(ci))
            tset[0] = 0
            for _ in gated("w_ple_gate", KC, 0, xrhs, xrk, AF.Sigmoid, (lambda ci: cvec[:, bp0 + ci:bp0 + ci + 1]),
                           "w_ple", KP, 0, (lambda kp: pT[:, kp, :]), (lambda kp: [("pT_", kp)]), D, fin_p):
                tset[0] = 0
            main_ln(3, want_x=False)
            stg(17)
            ydst = ya if inA else yb
            for m in range(4):
                sbi = m % 2
                for kq in range(KC // GK):
                    b = nb()
                    rk = []
                    for i in range(GK):
                        rk += yk(kq * GK + i)
                    pe(lambda e, b=b, kq=kq, m=m: [e.transpose(out=ps[b][:, i * 128:(i + 1) * 128], in_=yT[:, kq * GK + i, m * 128:(m + 1) * 128], identity=ident[:, :]) for i in range(GK)][-1],
                       rk + [("c", "ident")], PSK(b))
                    o_ap = stage[sbi][:, kq * GK * 128:(kq + 1) * GK * 128]
                    npg = max(1, GK // 2)
                    okeys = [("hp", sbi * SP + kq * npg + i_) for i_ in range(npg)]
                    if kq % 2 == 0:
                        act(lambda e, b=b, o_ap=o_ap: e.activation(out=o_ap, in_=ps[b][:, 0:GK * 128], func=AF.Identity), PSK(b), okeys)
                    else:
                        dve(lambda e, b=b, o_ap=o_ap: e.tensor_copy(out=o_ap, in_=ps[b][:, 0:GK * 128]), PSK(b), okeys)
                dma("act", ydst[prow + m * 128:prow + (m + 1) * 128, :], stage[sbi][:, :], stagek[sbi], [("out", pos0, m)])

    try:
        body()
    except _Stop:
        pass
    emit(nc, P, es)
    return nc


def emit(nc, P, es):
    ops = P.ops
    n = len(ops)
    sig = [False] * n
    for i, (eng, fn, deps, dma) in enumerate(ops):
        for d in deps:
            de, _, _, ddma = ops[d]
            if ddma:
                continue
            if (not dma) and de == eng and (eng == "pe" or not SAME_ENG_SYNC):
                continue
            sig[d] = True
    csem = {e: es.enter_context(nc.semaphore("c_" + e)) for e in ("pe", "act", "dve")}
    dsem = {q: [es.enter_context(nc.semaphore("d_%s%d" % (q, i))) for i in range(KR)] for q in ("sp", "act", "pool")}
    cnt = {"pe": 0, "act": 0, "dve": 0}
    dcnt = {"sp": 0, "act": 0, "pool": 0}
    semof = [None] * n
    streams = {e: [] for e in ("pe", "act", "dve", "pool", "sp")}
    for i, (eng, fn, deps, dma) in enumerate(ops):
        streams[eng].append(i)
        if dma:
            k = dcnt[eng]
            dcnt[eng] += 1
            semof[i] = (dsem[eng][k % KR], 16 * (k // KR + 1), k)
        elif sig[i]:
            cnt[eng] += 1
            semof[i] = (csem[eng], cnt[eng], -1)
    final = []
    for q in dsem:
        for r_ in range(KR):
            tot = (dcnt[q] - r_ + KR - 1) // KR if dcnt[q] > r_ else 0
            if tot > 0:
                final.append((dsem[q][r_], 16 * tot))

    def run(engname, e):
        waited = {}
        for i in streams[engname]:
            eng, fn, deps, dma = ops[i]
            need = {}
            for d in deps:
                de, _, _, ddma = ops[d]
                if (not ddma) and (not dma) and de == eng and (eng == "pe" or not SAME_ENG_SYNC):
                    continue
                s, v, _k = semof[d]
                key = id(s)
                if need.get(key, (None, 0))[1] < v:
                    need[key] = (s, v)
            if dma:
                s, v, k = semof[i]
                if k >= KR:
                    pv = 16 * (k // KR)
                    key = id(s)
                    if need.get(key, (None, 0))[1] < pv:
                        need[key] = (s, pv)
            for key, (s, v) in need.items():
                if waited.get(key, 0) < v:
                    e.wait_ge(s, v)
                    waited[key] = v
            ins = fn(e)
            if semof[i] is not None:
                ins.then_inc(semof[i][0], 16 if dma else 1)
        if engname == "sp":
            for s, v in final:
                e.wait_ge(s, v)

    with nc.Block() as block:
        @block.tensor
        def _(e):
            run("pe", e)

        @block.scalar
        def _(e):
            run("act", e)

        @block.vector
        def _(e):
            run("dve", e)

        @block.gpsimd
        def _(e):
            run("pool", e)

        @block.sync
        def _(e):
            run("sp", e)
    es.close()


def rel_bucket(rel):
    half = 16
    max_exact = 8
    ret = (rel > 0).astype(np.int32) * half
    n = np.abs(rel)
    large = max_exact + (np.log(np.maximum(n, 1) / max_exact) / np.log(1024 / max_exact) * (half - max_exact)).astype(np.int32)
    large = np.minimum(large, half - 1)
    return (ret + np.where(n < max_exact, n, large)).astype(np.int32)


def fm(v):
    return np.ascontiguousarray(np.asarray(v, np.float32).reshape(-1, 128).T)


def host_prep(cfg, inp):
    c = cfg
    f = lambda k: np.asarray(inp[k], np.float32)
    shared = {}
    for k in ("w_ff1_gate", "w_ff1_up", "w_ff1_down", "w_in", "w_conv_out", "w_attn_out", "w_out", "w_ff2_gate", "w_ff2_up", "w_ff2_down", "w_ple", "w_ple_gate"):
        shared[k] = np.ascontiguousarray(f(k)[0])
    ln_g, ln_b = f("ln_g")[0], f("ln_b")[0]
    cols = [fm(ln_g[i]) for i in range(4)] + [fm(ln_b[i]) for i in range(4)] + [fm(f("b_in")[0])]
    dw = f("conv_dw")[0]
    cols += [fm(dw[w]) for w in range(CONVW)]
    cols += [fm(f("conv_dw_b")[0]), fm(f("conv_ln_g")[0]), fm(f("conv_ln_b")[0]), fm(f("b_ple_gate")[0])]
    cvec = np.ascontiguousarray(np.concatenate(cols, axis=1))
    assert cvec.shape == (128, c.NCV), cvec.shape
    shared["cvec"] = cvec
    shared["ident"] = np.eye(128, dtype=np.float32)
    rb = f("rel_bias")
    bT = np.empty((c.NH, 128, 256), np.float32)
    ii = np.arange(128)[:, None]
    jj = np.arange(128)[None, :]
    for g, d in enumerate(DILS):
        for cch in range(2):
            rel = cch * 128 + ii - 64 - jj
            bk = rel_bucket(rel * d)
            ok = np.abs(rel) <= 64
            for hi in range(c.HPG):
                h = g * c.HPG + hi
                bT[h, :, cch * 128:(cch + 1) * 128] = np.where(ok, rb[bk, h], np.float32(NEG))
    shared["biasT"] = bT
    xp, xs_, pp, psm = f("x_prompt"), f("x_sample"), f("p_prompt")[0], f("p_sample")[0]
    maps = []
    for core in range(8):
        b, qtr = core // 4, core % 4
        g0 = qtr * c.QP - HALO
        xa = np.zeros((c.LA, c.D), np.float32)
        lo, hi_ = max(0, g0), min(c.SEQ, g0 + c.LA)
        xa[lo - g0:hi_ - g0] = xp[b, lo:hi_]
        valid = np.zeros(c.NPOS, bool)
        valid[lo - g0:hi_ - g0] = True
        valid[c.B0:c.B0 + c.QS] = True
        pmask = np.stack([np.broadcast_to(valid[p:p + T].astype(np.float32), (128, T)) for p in c.t1]).copy()
        kmask = np.zeros((len(c.t2), 128, c.NMK), np.float32)
        for ti, pos0 in enumerate(c.t2):
            for g, d in enumerate(DILS):
                nsub = T // d
                nwc = -(-(nsub + 128) // 128)
                for wc in range(nwc):
                    for r in range(d):
                        pos = pos0 - 64 * d + (wc * 128 + np.arange(128)) * d + r
                        ok = (pos < c.NPOS) & valid[np.minimum(pos, c.NPOS - 1)]
                        kmask[ti, :, c.moff[g] + wc * d + r] = np.where(ok, 0.0, NEG)
        m = dict(shared)
        m.update(xa=xa, xb=np.ascontiguousarray(xs_[core]), pa=np.ascontiguousarray(pp[b, qtr * c.QP:(qtr + 1) * c.QP]),
                 pb=np.ascontiguousarray(psm[core]), pmask=pmask, kmask=kmask)
        maps.append(m)
    return maps


def run_cfg(cfg, inp, limit=99):
    nc = build(cfg, limit)
    maps = host_prep(cfg, inp)
    res = run_bass_kernel_spmd(nc, maps, core_ids=list(range(8)))
    yp = np.empty((2, cfg.SEQ, cfg.D), np.float32)
    ysm = np.empty((8, cfg.QS, cfg.D), np.float32)
    for core in range(8):
        b, qtr = core // 4, core % 4
        yp[b, qtr * cfg.QP:(qtr + 1) * cfg.QP] = res.results[core]["ya"]
        ysm[core] = res.results[core]["yb"]
    return yp, ysm


def kernel(**inputs):
    return run_cfg(Cfg(), inputs)
```

```python
import math
from contextlib import ExitStack

import numpy as np
import concourse.bass as bass
import concourse.mybir as mybir
from concourse.bass_utils import run_bass_kernel_spmd

F32 = mybir.dt.float32
BF16 = mybir.dt.bfloat16
AF = mybir.ActivationFunctionType
ALU = mybir.AluOpType

ALPHA = 2.0 ** 0.25
EPS = 1e-5
NEG = -30000.0
T = 512
HALO = 1024
DILS = (1, 4, 16)
CONVW = 31
SCALE = 128 ** -0.5
NSLOT = 3
KR = 8
SAME_ENG_SYNC = False
NTMP = 9


class Cfg:
    def __init__(self, D=4096, DFF=11008, DCONV=2048, HPG=8, DPLE=256, SEQ=8192, QS=2048, HG=44):
        self.D, self.DFF, self.DCONV, self.HPG, self.DPLE, self.SEQ, self.QS, self.HG = D, DFF, DCONV, HPG, DPLE, SEQ, QS, HG
        self.KC = D // 128
        self.NJ = DFF // 128
        self.NCC = DCONV // 128
        self.KP = DPLE // 128
        self.NH = 3 * HPG
        self.DATT = self.NH * 128
        self.C_Q = 2 * DCONV
        self.C_K = self.C_Q + self.DATT
        self.C_V = self.C_K + self.DATT
        self.C_GATE = self.C_V + self.DATT
        self.DIN = self.C_GATE + 2 * D
        self.QP = SEQ // 4
        self.LA = self.QP + 2 * HALO
        self.B0 = self.LA + HALO
        self.NPOS = self.B0 + QS + HALO
        self.t1 = [p for p in range(0, self.LA, T)] + [self.B0 + i * T for i in range(QS // T)]
        self.t2 = [HALO + i * T for i in range(self.QP // T)] + [self.B0 + i * T for i in range(QS // T)]
        o = 0
        self.cv = {}
        for name, n in (("g", 4 * self.KC), ("b", 4 * self.KC), ("bin", self.DIN // 128), ("dw", CONVW * self.NCC),
                        ("dwb", self.NCC), ("cg", self.NCC), ("cb", self.NCC), ("bpg", self.KC)):
            self.cv[name] = o
            o += n
        self.NCV = o
        self.moff = []
        o = 0
        for d in DILS:
            self.moff.append(o)
            nsub = T // d
            o += d * (-(-(nsub + 128) // 128))
        self.NMK = o

    def own(self, p):
        return (HALO <= p < HALO + self.QP) or p >= self.B0

    def glu(self, p):
        return (HALO - T <= p < HALO + self.QP + T) or p >= self.B0


class Prog:
    def __init__(self):
        self.ops = []
        self.lastw = {}
        self.rdc = {}
        self.rdd = {}

    def add(self, eng, fn, reads=(), writes=(), dma=False):
        i = len(self.ops)
        deps = set()
        for r in reads:
            w = self.lastw.get(r)
            if w is not None:
                deps.add(w)
        for r in writes:
            w = self.lastw.get(r)
            if w is not None:
                deps.add(w)
            d = self.rdc.get(r)
            if d:
                deps.update(d.values())
            l = self.rdd.get(r)
            if l:
                deps.update(l)
        self.ops.append((eng, fn, deps, dma))
        for r in reads:
            if dma:
                self.rdd.setdefault(r, []).append(i)
            else:
                self.rdc.setdefault(r, {})[eng] = i
        for r in writes:
            self.lastw[r] = i
            self.rdc[r] = {}
            self.rdd[r] = []
        return i


class _Stop(Exception):
    pass


def build(cfg, limit=99):
    c = cfg

    def stg(n):
        if n > limit:
            raise _Stop()
    D, KC, NJ, NCC, KP, HPG, NH = c.D, c.KC, c.NJ, c.NCC, c.KP, c.HPG, c.NH
    NPOS = c.NPOS
    nc = bass.Bass("TRN2", target_bir_lowering=False)
    P = Prog()

    def din(name, shape, dt=F32):
        return nc.dram_tensor(name, list(shape), dt, kind="ExternalInput").ap()

    def dint(name, shape, dt):
        return nc.dram_tensor(name, list(shape), dt, kind="Internal").ap()

    xa = din("xa", [c.LA, D])
    xb = din("xb", [c.QS, D])
    pa = din("pa", [c.QP, c.DPLE])
    pb = din("pb", [c.QS, c.DPLE])
    wshapes = {
        "w_ff1_gate": (D, c.DFF), "w_ff1_up": (D, c.DFF), "w_ff1_down": (c.DFF, D), "w_in": (D, c.DIN),
        "w_conv_out": (c.DCONV, D), "w_attn_out": (HPG * 128, D), "w_out": (D, D),
        "w_ff2_gate": (D, c.DFF), "w_ff2_up": (D, c.DFF), "w_ff2_down": (c.DFF, D),
        "w_ple": (c.DPLE, D), "w_ple_gate": (D, D),
    }
    wf = {k: din(k, v) for k, v in wshapes.items()}
    wb = {k: dint(k + "_bf", v, BF16) for k, v in wshapes.items()}
    cvec_d = din("cvec", [128, c.NCV])
    biasT_d = din("biasT", [NH, 128, 256])
    kmask_d = din("kmask", [len(c.t2), 128, c.NMK])
    pmask_d = din("pmask", [len(c.t1), 128, T])
    ident_d = din("ident", [128, 128])
    ya = nc.dram_tensor("ya", [c.QP, D], F32, kind="ExternalOutput").ap()
    yb = nc.dram_tensor("yb", [c.QS, D], F32, kind="ExternalOutput").ap()
    x1s = dint("x1s", [D, NPOS], F32)
    x1b = dint("x1b", [D, NPOS], BF16)
    kts = dint("kts", [NH, 128, NPOS], BF16)
    vs = dint("vs", [NH, NPOS + 2048, 128], BF16)
    hg = dint("hg", [c.DCONV, NPOS], F32)

    es = ExitStack()

    def sb(name, shape, dt):
        return es.enter_context(nc.sbuf_tensor(name, list(shape), dt))

    HPAGES = max(c.HG, 2 * (D // 256), KC)
    ybuf = sb("ybuf", [128, 16384], F32)
    xT = sb("xT", [128, KC, T], BF16)
    hbuf = sb("hbuf", [128, HPAGES * 512], BF16)
    wring = [sb("wr%d" % i, [128, 8, 512], BF16) for i in range(NSLOT)]
    tmp = sb("tmp", [128, NTMP, 512], F32)
    cvec = sb("cvec_s", [128, c.NCV], F32)
    cv2 = sb("cv2", [128, 8 * KC + NH], F32)
    ident = sb("ident_s", [128, 128], F32)
    ones32 = sb("ones32", [128, 128], F32)
    onesb = sb("onesb", [128, 128], BF16)
    epsc = sb("epsc", [128, 1], F32)
    bt = sb("bt", [128, 2, 256], F32)
    km = sb("km", [128, 2, c.NMK], F32)
    pmk = sb("pmk", [128, T], F32)
    hw = sb("hw", [128, 2, T + 32], F32)
    pT = sb("pT", [128, KP, T], BF16)
    vst = sb("vst", [128, 4, 512], BF16)
    ps = [es.enter_context(nc.psum_tensor("ps%d" % i, [128, 512], F32)) for i in range(8)]

    yT = ybuf[:, 0:KC * 512].rearrange("p (k t) -> p k t", t=512)

    def YK(lo_kb, hi_kb):
        return [("yp", i) for i in range(int(lo_kb), int(math.ceil(hi_kb)))]

    def yk(kc):
        return [("yp", 2 * kc), ("yp", 2 * kc + 1)]

    def ybf(lo_kb, n_kb):
        return ybuf[:, lo_kb * 256:(lo_kb + n_kb) * 256].bitcast(BF16)

    attnT = ybf(0, HPG).rearrange("p (h t) -> p h t", t=512)
    qT = ybf(8, NH).rearrange("p (h t) -> p h t", t=512)
    kt = [ybf(32, 5), ybf(37, 5)]
    ktk = [YK(32, 37), YK(37, 42)]
    vch = [ybf(42, 8), ybf(50, 8)]
    vchk = [YK(42, 50), YK(50, 58)]
    cvf = ybuf[:, 8 * 256:(8 + 2 * NCC) * 256].rearrange("p (k t) -> p k t", t=512)
    cvs = ybf(40, NCC).rearrange("p (k t) -> p k t", t=512)

    hT = hbuf[:, :].rearrange("p (k t) -> p k t", t=512)

    def hk(j):
        return [("hp", j)]
    SP = D // 256
    stage = [hbuf[:, sbi * SP * 512:(sbi + 1) * SP * 512].bitcast(F32) for sbi in range(2)]
    stagek = [[("hp", sbi * SP + i) for i in range(SP)] for sbi in range(2)]
    mT = hT

    def tk(i):
        return [("t", i)]

    def cvc(name, i=0):
        return cvec[:, c.cv[name] + i:c.cv[name] + i + 1]

    def PSK(b):
        return [("ps", b, q) for q in range(4)]

    nbank = [0]

    def nb():
        b = nbank[0] % 8
        nbank[0] += 1
        return b

    def pe(fn, r, w):
        return P.add("pe", fn, r, w)

    def act(fn, r, w):
        return P.add("act", fn, r, w)

    def dve(fn, r, w):
        return P.add("dve", fn, r, w)

    def dma(q, out, in_, r, w):
        return P.add(q, lambda e: e.dma_start(out=out, in_=in_), r, w, dma=True)

    CK = [("c", "consts")]

    dma("act", cvec[:, :], cvec_d[:, :], [], [("c", "cvec")])
    dma("act", ident[:, :], ident_d[:, :], [], [("c", "ident")])
    dve(lambda e: e.memset(ones32[:, :], 1.0), [], [("c", "ones32")])
    dve(lambda e: e.memset(onesb[:, :], 1.0), [], [("c", "onesb")])
    dve(lambda e: e.memset(epsc[:, :], EPS), [], [("c", "eps")])
    for i in range(4):
        s_ = ALPHA if i < 3 else 1.0
        dve(lambda e, i=i, s_=s_: e.tensor_scalar(cv2[:, i * KC:(i + 1) * KC], cvec[:, c.cv["g"] + i * KC:c.cv["g"] + (i + 1) * KC], s_, None, ALU.mult),
            [("c", "cvec")], [("c", "cv2", i)])
        dve(lambda e, i=i, s_=s_: e.tensor_scalar(cv2[:, (4 + i) * KC:(5 + i) * KC], cvec[:, c.cv["b"] + i * KC:c.cv["b"] + (i + 1) * KC], s_, None, ALU.mult),
            [("c", "cvec")], [("c", "cv2", 4 + i)])
    qc0 = c.cv["bin"] + c.C_Q // 128
    dve(lambda e: e.tensor_scalar(cv2[:, 8 * KC:8 * KC + NH], cvec[:, qc0:qc0 + NH], SCALE, None, ALU.mult), [("c", "cvec")], [("c", "bqs")])
    CALL = [("c", "cvec"), ("c", "bqs")] + [("c", "cv2", i) for i in range(8)]

    zb = tmp[:, 0, :].bitcast(BF16)
    dve(lambda e: e.memset(tmp[:, 0, :], 0.0), [], tk(0))
    gaps = [c.LA, c.B0 + c.QS]
    for g0_ in (gaps if limit >= -1 else []):
        tl = [g0_ // T, g0_ // T + 1]
        for h in range(NH):
            dma("pool", kts[h, :, g0_:g0_ + HALO], zb[:, 0:HALO], tk(0), [("kt", t_, h) for t_ in tl])
            dma("pool", vs[h, g0_:g0_ + HALO, :].rearrange("(p a) e -> p (a e)", p=128), zb[:, 0:HALO], tk(0), [("vs", t_, h, m_) for t_ in tl for m_ in range(4)])
    for hp_, tix in (((c.B0 - 16, (c.B0 - 16) // T), (c.B0 + c.QS, (c.B0 + c.QS) // T)) if limit >= -1 else []):
        dma("pool", hg[:, hp_:hp_ + 16].rearrange("(k p) t -> p k t", p=128),
            tmp[:, 0, 0:NCC * 16].rearrange("p (k t) -> p k t", t=16), tk(0), [("hg", tix, cg_) for cg_ in range((NCC + 3) // 4)])

    ROWW = ("w_ff1_down", "w_ff2_down")
    wblk = {}
    for name in ROWW:
        wblk[name] = max(16, ((2 * 1024 * 1024) // wshapes[name][1]) // 16 * 16)

    def conv_cols(name, c0, c1):
        for cs in range(c0 // 512, (c1 + 511) // 512):
            a, b_ = cs * 512, min(wshapes[name][1], cs * 512 + 512)
            dma("pool", wb[name][:, a:b_], wf[name][:, a:b_], [], [("wb", name, cs)])

    def conv_rows(name):
        Kr, N = wshapes[name]
        R = wblk[name]
        for bi, r0 in enumerate(range(0, Kr, R)):
            r1 = min(Kr, r0 + R)
            dma("pool", wb[name][r0:r1, :], wf[name][r0:r1, :], [], [("wb", name, bi)])

    def conv_pair(n1, n2):
        N = wshapes[n1][1]
        for cs in range((N + 511) // 512):
            conv_cols(n1, cs * 512, min(N, cs * 512 + 512))
            conv_cols(n2, cs * 512, min(N, cs * 512 + 512))

    conv_pair("w_ff1_gate", "w_ff1_up")
    conv_rows("w_ff1_down")
    conv_cols("w_in", c.C_K, c.C_GATE)
    conv_cols("w_in", 0, c.C_Q)
    deferred = {
        0: [lambda: conv_cols("w_in", c.C_Q, c.C_K), lambda: conv_cols("w_in", c.C_GATE, c.DIN)],
        1: [lambda: conv_cols("w_conv_out", 0, D), lambda: conv_cols("w_attn_out", 0, D), lambda: conv_cols("w_out", 0, D)],
        2: [lambda: conv_pair("w_ff2_gate", "w_ff2_up")],
        3: [lambda: conv_rows("w_ff2_down")],
        4: [lambda: conv_cols("w_ple_gate", 0, D), lambda: conv_cols("w_ple", 0, D)],
    }

    def wbkeys(name, kc0, R, c0, W):
        if name in ROWW:
            Rb = wblk[name]
            return [("wb", name, bi) for bi in range((kc0 * 128) // Rb, ((kc0 + R) * 128 - 1) // Rb + 1)]
        return [("wb", name, cs) for cs in range(c0 // 512, (c0 + W - 1) // 512 + 1)]

    wn = [0]

    def wtile(name, kc0, R, c0, W):
        s = wn[0] % NSLOT
        wn[0] += 1
        src = wb[name].rearrange("(k p) n -> p k n", p=128)[:, kc0:kc0 + R, c0:c0 + W]
        dma("sp", wring[s][:, 0:R, 0:W], src, wbkeys(name, kc0, R, c0, W), [("w", s)])
        return s

    def mm_fm(name, k0, k1, col0, W, rhs, rkeys, banks, N=T):
        ncc = W // 128
        for kc0 in range(k0, k1, 8):
            R = min(8, k1 - kc0)
            s = wtile(name, kc0, R, col0, W)

            def fn(e, s=s, kc0=kc0, R=R):
                ins = None
                for r in range(R):
                    for cc in range(ncc):
                        ins = e.matmul(ps[banks[cc]][:, 0:N], lhsT=wring[s][:, r, cc * 128:(cc + 1) * 128], rhs=rhs(kc0 + r),
                                       start=(kc0 + r == k0), stop=(kc0 + r == k1 - 1))
                return ins
            reads = [("w", s)]
            for r in range(R):
                reads += rkeys(kc0 + r)
            writes = []
            for b in banks[:ncc]:
                writes += PSK(b)
            import os as _os2
            if not ("M0" in _os2.environ.get("SKIP", "") and name == "w_in" and col0 >= c.C_Q and col0 < c.C_K):
                pe(fn, reads, writes)

    def proj_fm(name, k0, k1, col0, ncols, rhs, rkeys, evac):
        for g0 in range(0, ncols, 512):
            W = min(512, ncols - g0)
            banks = [nb() for _ in range(W // 128)]
            mm_fm(name, k0, k1, col0 + g0, W, rhs, rkeys, banks)
            for cc in range(W // 128):
                evac(g0 // 128 + cc, banks[cc])

    tset = [0]

    def gated(n1, k1, col1, rhs1, rk1, func, bias1, n2, k2, col2, rhs2, rk2, ncols, fin, post=None):
        for g0 in range(0, ncols, 512):
            W = min(512, ncols - g0)
            ncc = W // 128
            ts = [(tset[0] % 2) * 4 + i for i in range(4)]
            tset[0] += 1
            b1 = [nb() for _ in range(ncc)]
            mm_fm(n1, 0, k1, col1 + g0, W, rhs1, rk1, b1)
            for cc in range(ncc):
                ci = g0 // 128 + cc
                bia = bias1(ci) if bias1 is not None else 0.0
                act(lambda e, t_=ts[cc], b_=b1[cc], bia=bia: e.activation(out=tmp[:, t_, :], in_=ps[b_][:, :], func=func, bias=bia),
                    PSK(b1[cc]) + CALL, tk(ts[cc]))
                if post is not None:
                    post(ts[cc])
            b2 = [nb() for _ in range(ncc)]
            mm_fm(n2, 0, k2, col2 + g0, W, rhs2, rk2, b2)
            for cc in range(ncc):
                fin(g0 // 128 + cc, cc, b2[cc], ts[cc])
            yield g0

    xrhs = (lambda kc: xT[:, kc, :])
    xrk = (lambda kc: [("xT", kc)])

    def ffn(wg, wu, wd):
        for j0 in range(0, NJ, c.HG):
            j1 = min(NJ, j0 + c.HG)

            def fin(ci, cc, b2, tslot, j0=j0):
                dve(lambda e: e.tensor_tensor(out=hT[:, ci, :], in0=ps[b2][:, :], in1=tmp[:, tslot, :], op=ALU.mult),
                    PSK(b2) + tk(tslot), hk(ci))
            for _ in gated(wg, KC, j0 * 128, xrhs, xrk, AF.Silu, None, wu, KC, j0 * 128, xrhs, xrk, (j1 - j0) * 128, fin):
                pass

            def evac(m, b):
                dve(lambda e: e.scalar_tensor_tensor(out=yT[:, m, :], in0=ps[b][:, :], scalar=0.5, in1=yT[:, m, :], op0=ALU.mult, op1=ALU.add),
                    PSK(b) + yk(m), yk(m))
            proj_fm(wd, j0, j1, 0, D, (lambda j, j0=j0: hT[:, j - j0, :]), (lambda j, j0=j0: hk(j - j0)), evac)

    MEAN, RSTD, M2 = 0, 1, 2

    def layer_norm(z, zkeys, NCH, Dn, gcol, bcol, sgcol, sbcol, y_out, xo, func):
        bs_, bq_ = nb(), nb()
        for kc in range(NCH):
            i = 3 + kc % 2
            act(lambda e, kc=kc, i=i: e.activation(out=tmp[:, i, :], in_=z(kc), func=AF.Square), zkeys(kc), tk(i))
            pe(lambda e, kc=kc, i=i: (e.matmul(ps[bs_][:, :], lhsT=ones32[:, :], rhs=z(kc), start=(kc == 0), stop=(kc == NCH - 1)),
                                      e.matmul(ps[bq_][:, :], lhsT=ones32[:, :], rhs=tmp[:, i, :], start=(kc == 0), stop=(kc == NCH - 1)))[-1],
               zkeys(kc) + tk(i) + [("c", "ones32")], PSK(bs_) + PSK(bq_))
        act(lambda e: e.activation(out=tmp[:, MEAN, :], in_=ps[bs_][:, :], func=AF.Identity, scale=1.0 / Dn), PSK(bs_), tk(MEAN))
        dve(lambda e: e.tensor_tensor(out=tmp[:, M2, :], in0=tmp[:, MEAN, :], in1=tmp[:, MEAN, :], op=ALU.mult), tk(MEAN), tk(M2))
        dve(lambda e: e.scalar_tensor_tensor(out=tmp[:, RSTD, :], in0=ps[bq_][:, :], scalar=1.0 / Dn, in1=tmp[:, M2, :], op0=ALU.mult, op1=ALU.subtract),
            PSK(bq_) + tk(M2), tk(RSTD))
        act(lambda e: e.activation(out=tmp[:, M2, :], in_=tmp[:, RSTD, :], func=AF.Sqrt, bias=epsc[:, 0:1]), tk(RSTD) + [("c", "eps")], tk(M2))
        dve(lambda e: e.reciprocal(out=tmp[:, RSTD, :], in_=tmp[:, M2, :]), tk(M2), tk(RSTD))
        for kc in range(NCH):
            i1 = 5 + kc % 2
            i2 = 7 + kc % 2
            dve(lambda e, kc=kc, i1=i1: e.tensor_tensor(out=tmp[:, i1, :], in0=z(kc), in1=tmp[:, MEAN, :], op=ALU.subtract), zkeys(kc) + tk(MEAN), tk(i1))
            dve(lambda e, i1=i1, i2=i2: e.tensor_tensor(out=tmp[:, i2, :], in0=tmp[:, i1, :], in1=tmp[:, RSTD, :], op=ALU.mult), tk(i1) + tk(RSTD), tk(i2))
            if xo is not None:
                o_ap, o_k = xo(kc)
                act(lambda e, kc=kc, i2=i2, o_ap=o_ap: e.activation(out=o_ap, in_=tmp[:, i2, :], func=func, bias=bcol(kc), scale=gcol(kc)), tk(i2) + CALL, o_k)
            if y_out is not None:
                o_ap, o_k = y_out(kc)
                act(lambda e, kc=kc, i2=i2, o_ap=o_ap: e.activation(out=o_ap, in_=tmp[:, i2, :], func=AF.Identity, bias=sbcol(kc), scale=sgcol(kc)), tk(i2) + CALL, o_k)

    def main_ln(idx, want_x=True):
        g0 = c.cv["g"] + idx * KC
        b0 = c.cv["b"] + idx * KC
        layer_norm(lambda kc: yT[:, kc, :], yk, KC, D,
                   lambda kc: cvec[:, g0 + kc:g0 + kc + 1], lambda kc: cvec[:, b0 + kc:b0 + kc + 1],
                   lambda kc: cv2[:, idx * KC + kc:idx * KC + kc + 1], lambda kc: cv2[:, (4 + idx) * KC + kc:(4 + idx) * KC + kc + 1],
                   lambda kc: (yT[:, kc, :], yk(kc)),
                   (lambda kc: (xT[:, kc, :], [("xT", kc)])) if want_x else None, AF.Identity)

    GK = min(4, KC)

    def body():
        stg(1)
        for ti, pos0 in enumerate(c.t1):
            inA = pos0 < c.LA
            tix = pos0 // T
            import os as _os
            _sk = _os.environ.get("SKIP", "")
            if "P" not in _sk:
                dma("act", pmk[:, :], pmask_d[ti], [], [("pmk",)])
            for m in range(4):
                sbi = m % 2
                src = xa[pos0 + m * 128:pos0 + (m + 1) * 128, :] if inA else xb[pos0 - c.B0 + m * 128:pos0 - c.B0 + (m + 1) * 128, :]
                dma("act", stage[sbi][:, :], src, [], stagek[sbi])
                for kq in range(KC // GK):
                    b = nb()
                    pe(lambda e, b=b, kq=kq, sbi=sbi: [e.transpose(out=ps[b][:, i * 128:(i + 1) * 128], in_=stage[sbi][:, (kq * GK + i) * 128:(kq * GK + i + 1) * 128], identity=ident[:, :])
                                                        for i in range(GK)][-1], stagek[sbi] + [("c", "ident")], PSK(b))
                    yks = []
                    xks = []
                    for i in range(GK):
                        yks += yk(kq * GK + i)
                        xks += [("xT", kq * GK + i)]
                    pv = (lambda b: ps[b][:, 0:GK * 128].rearrange("p (i t) -> p i t", i=GK))
                    if "A" not in _sk:
                        act(lambda e, b=b, kq=kq, m=m: e.activation(out=yT[:, kq * GK:(kq + 1) * GK, m * 128:(m + 1) * 128], in_=pv(b), func=AF.Identity, scale=ALPHA), PSK(b), yks)
                    if "D" not in _sk:
                        dve(lambda e, b=b, kq=kq, m=m: e.tensor_copy(out=xT[:, kq * GK:(kq + 1) * GK, m * 128:(m + 1) * 128], in_=pv(b)), PSK(b) + yks, xks)
            stg(2)
            ffn("w_ff1_gate", "w_ff1_up", "w_ff1_down")
            stg(3)
            main_ln(0)
            stg(4)
            if c.own(pos0):
                for k0 in range(0, KC, 8):
                    k1 = min(KC, k0 + 8)
                    ks = []
                    xs = []
                    for kc in range(k0, k1):
                        ks += yk(kc)
                        xs += [("xT", kc)]
                    dma("pool", x1s[k0 * 128:k1 * 128, pos0:pos0 + T].rearrange("(k p) t -> p k t", p=128), yT[:, k0:k1, :], ks, [("x1s", pos0, k0)])
                    dma("pool", x1b[k0 * 128:k1 * 128, pos0:pos0 + T].rearrange("(k p) t -> p k t", p=128), xT[:, k0:k1, :], xs, [("x1b", pos0, k0)])
            stg(5)
            kb0 = c.cv["bin"] + c.C_K // 128
            for g0 in range(0, c.DATT, 512):
                W = min(512, c.DATT - g0)
                nh_ = W // 128
                banks = [nb() for _ in range(nh_)]
                mm_fm("w_in", 0, KC, c.C_K + g0, W, xrhs, xrk, banks)
                h0 = g0 // 128
                for cc in range(nh_):
                    act(lambda e, cc=cc, b=banks[cc], h=h0 + cc: e.activation(out=vst[:, cc, :], in_=ps[b][:, :], func=AF.Identity, bias=cvec[:, kb0 + h:kb0 + h + 1]),
                        PSK(banks[cc]) + CALL, [("vst", cc)])
                dma("pool", kts[h0:h0 + nh_, :, pos0:pos0 + T].rearrange("h p t -> p h t"), vst[:, 0:nh_, :], [("vst", i) for i in range(nh_)],
                    [("kt", tix, h0 + i) for i in range(nh_)])
            stg(6)
            for g0 in range(0, c.DATT, 512):
                W = min(512, c.DATT - g0)
                nh_ = W // 128
                banks = [nb() for _ in range(4)]
                for kc0 in range(0, KC, 8):
                    R = min(8, KC - kc0)
                    s = wtile("w_in", kc0, R, c.C_V + g0, W)

                    def fn(e, s=s, kc0=kc0, R=R, banks=banks, W=W):
                        ins = None
                        for r in range(R):
                            for m in range(4):
                                ins = e.matmul(ps[banks[m]][:, 0:W], lhsT=xT[:, kc0 + r, m * 128:(m + 1) * 128], rhs=wring[s][:, r, 0:W],
                                               start=(kc0 + r == 0), stop=(kc0 + r == KC - 1))
                        return ins
                    rd = [("w", s)] + [("xT", kc0 + r) for r in range(R)]
                    wr = []
                    for b in banks:
                        wr += PSK(b)
                    pe(fn, rd, wr)
                for m in range(4):
                    if m % 2 == 0:
                        dve(lambda e, m=m, b=banks[m], W=W: e.tensor_copy(out=vst[:, m, 0:W], in_=ps[b][:, 0:W]), PSK(banks[m]), [("vst", m)])
                    else:
                        act(lambda e, m=m, b=banks[m], W=W: e.activation(out=vst[:, m, 0:W], in_=ps[b][:, 0:W], func=AF.Identity), PSK(banks[m]), [("vst", m)])
                h0 = g0 // 128
                for m in range(4):
                    dma("pool", vs[h0:h0 + nh_, pos0 + m * 128:pos0 + (m + 1) * 128, :].rearrange("h p e -> p h e"),
                        vst[:, m, 0:W].rearrange("p (h e) -> p h e", e=128), [("vst", m)], [("vs", tix, h0 + i, m) for i in range(nh_)])
            stg(7)
            if c.glu(pos0):
                gb0 = c.cv["bin"]

                def post(tslot):
                    dve(lambda e: e.tensor_tensor(out=tmp[:, tslot, :], in0=tmp[:, tslot, :], in1=pmk[:, :], op=ALU.mult), tk(tslot) + [("pmk",)], tk(tslot))

                def fin(ci, cc, b2, tslot):
                    oslot = (tslot + 4) % 8
                    dve(lambda e: e.scalar_tensor_tensor(out=tmp[:, oslot, :], in0=ps[b2][:, :], scalar=cvec[:, gb0 + ci:gb0 + ci + 1], in1=tmp[:, tslot, :], op0=ALU.add, op1=ALU.mult),
                        PSK(b2) + tk(tslot) + CALL, tk(oslot))
                for g0 in gated("w_in", KC, c.DCONV, xrhs, xrk, AF.Sigmoid, (lambda ci: cvec[:, gb0 + NCC + ci:gb0 + NCC + ci + 1]),
                                "w_in", KC, 0, xrhs, xrk, c.DCONV, fin, post):
                    ncc = min(512, c.DCONV - g0) // 128
                    ci0 = g0 // 128
                    ob = (((tset[0] - 1) % 2) * 4 + 4) % 8
                    dma("pool", hg[ci0 * 128:(ci0 + ncc) * 128, pos0:pos0 + T].rearrange("(k p) t -> p k t", p=128), tmp[:, ob:ob + ncc, :],
                        [("t", ob + i) for i in range(ncc)], [("hg", tix, g0 // 512)])
                    tset[0] += 1

            for f_ in deferred.get(ti, []):
                f_()
        stg(8)
        for ti, pos0 in enumerate(c.t2):
            inA = pos0 < c.LA
            tix = pos0 // T
            for k0 in range(0, KC, 8):
                k1 = min(KC, k0 + 8)
                dma("pool", xT[:, k0:k1, :], x1b[k0 * 128:k1 * 128, pos0:pos0 + T].rearrange("(k p) t -> p k t", p=128),
                    [("x1b", pos0, k0)], [("xT", kc) for kc in range(k0, k1)])
            kmb = ti % 2
            dma("pool", km[:, kmb, :], kmask_d[ti], [], [("km", kmb)])
            stg(9)
            qb0 = c.cv["bin"] + c.C_Q // 128

            def qevac(ci, b):
                d = DILS[ci // HPG]
                o_ = qT[:, ci, :] if d == 1 else qT[:, ci, :].rearrange("p (r j) -> p j r", r=d)
                i_ = ps[b][:, :] if d == 1 else ps[b][:, :].rearrange("p (j r) -> p j r", r=d)
                if "Q1" in _sk:
                    o_, i_ = qT[:, ci, :], ps[b][:, :]
                if "Q2" in _sk:
                    o_, i_ = vst[:, ci % 4, :], ps[b][:, :]
                if "Q4" in _sk:
                    return
                if "Q3" in _sk:
                    act(lambda e: e.activation(out=o_, in_=i_, func=AF.Identity, bias=cvec[:, 0:1]), PSK(b) + CALL, YK(8 + ci, 9 + ci))
                else:
                    act(lambda e: e.activation(out=o_, in_=i_, func=AF.Identity, bias=cv2[:, 8 * KC + ci:8 * KC + ci + 1], scale=SCALE), PSK(b) + CALL, YK(8 + ci, 9 + ci))
            proj_fm("w_in", 0, KC, c.C_Q, c.DATT, xrhs, xrk, qevac)
            stg(10)
            akeys = [("tS", i_) for i_ in range(8)] + [("pT", i_) for i_ in range(8)]
            dve(lambda e: e.memset(tmp[0:1, 0, 0:1], 0.0), [], tk(0) + tk(1) + tk(2) + akeys)
            nload = [0]
            for hi in range(HPG):
                bU = [0, 1, 2]
                bZ = [3, 4, 5]
                for g, d in enumerate(DILS):
                    head = g * HPG + hi
                    nsub = T // d
                    nq = min(128, nsub)
                    nqb = nsub // nq
                    WLs = nsub + 128
                    nwc = -(-WLs // 128)
                    lb = nload[0] % 2
                    nload[0] += 1
                    w0 = pos0 - 64 * d
                    WL = T + 128 * d
                    tl = list(range(w0 // T, (w0 + WL - 1) // T + 1))
                    dma("pool", kt[lb][:, 0:WL], kts[head, :, w0:w0 + WL], [("kt", t_, head) for t_ in tl], ktk[lb])
                    vv = vch[lb][:, 0:nwc * d * 128].rearrange("p (w r e) -> p w r e", r=d, e=128)
                    dma("pool", vch[lb][:, 0:nwc * d * 128].rearrange("p (w x) -> p w x", w=nwc),
                        vs[head, w0:w0 + nwc * 128 * d, :].rearrange("(w i r) e -> i w (r e)", w=nwc, r=d),
                        [("vs", t_, head, m_) for t_ in tl for m_ in range(4)], vchk[lb])
                    vkeys = vchk[lb]
                    dma("pool", bt[:, lb, :], biasT_d[head], [], [("bt", lb)])
                    ktv = kt[lb][:, 0:WL].rearrange("p (a r) -> p a r", r=d)
                    pend = []
                    blk = 0
                    for r in range(d):
                        for qb in range(nqb):
                            col0 = r * nsub + qb * nq
                            cur = []
                            for cch in range(2):
                                wc = (qb * nq) // 128 + cch
                                nk = min(128, nq + 128 - cch * 128)
                                sq = (blk * 2 + cch) % 8
                                sbk, sqq = 6 + cch, sq % 4
                                S = ps[sbk][0:nk, 0:nq]
                                tS = tmp[0:nk, 0, sqq * 128:sqq * 128 + nq] if sq < 4 else tmp[0:nk, 1, sqq * 128:sqq * 128 + nq]
                                tSk = ("tS", sq)
                                pTt = tmp[:, 2, :].bitcast(BF16)[0:nk, sq * 128:sq * 128 + nq]
                                pk = ("pT", sq)
                                pe(lambda e, S=S, wc=wc, nk=nk, r=r, col0=col0, nq=nq, ktv=ktv, hh=head: e.matmul(S, lhsT=ktv[:, wc * 128:wc * 128 + nk, r], rhs=qT[:, hh, col0:col0 + nq], start=True, stop=True),
                                   ktk[lb] + YK(8 + head, 9 + head), PSK(sbk))
                                dve(lambda e, S=S, tS=tS, nk=nk, cch=cch, nq=nq, lb=lb: e.tensor_tensor(out=tS, in0=S, in1=bt[0:nk, lb, cch * 128:cch * 128 + nq], op=ALU.add),
                                    PSK(sbk) + [("bt", lb)], [tSk])
                                mcol = c.moff[g] + wc * d + r
                                act(lambda e, tS=tS, pTt=pTt, nk=nk, mcol=mcol, kmb=kmb: e.activation(out=pTt, in_=tS, func=AF.Exp, bias=km[0:nk, kmb, mcol:mcol + 1]),
                                    [tSk, ("km", kmb)], [pk])
                                cur.append((pTt, pk, nk, wc, cch, col0, nq, r))
                            for (pTt, pk, nk, wc, cch, col0_, nq_, r_) in pend:
                                pe(lambda e, pTt=pTt, nk=nk, wc=wc, cch=cch, col0_=col0_, nq_=nq_, r_=r_, vv=vv, g=g: (
                                    e.matmul(ps[bU[g]][:, col0_:col0_ + nq_], lhsT=vv[0:nk, wc, r_, :], rhs=pTt, start=(cch == 0), stop=(cch == 1)),
                                    e.matmul(ps[bZ[g]][:, col0_:col0_ + nq_], lhsT=onesb[0:nk, :], rhs=pTt, start=(cch == 0), stop=(cch == 1)))[-1],
                                   [pk, ("c", "onesb")] + vkeys, PSK(bU[g]) + PSK(bZ[g]))
                            pend = cur
                            blk += 1
                    for (pTt, pk, nk, wc, cch, col0_, nq_, r_) in pend:
                        pe(lambda e, pTt=pTt, nk=nk, wc=wc, cch=cch, col0_=col0_, nq_=nq_, r_=r_, vv=vv, g=g: (
                            e.matmul(ps[bU[g]][:, col0_:col0_ + nq_], lhsT=vv[0:nk, wc, r_, :], rhs=pTt, start=(cch == 0), stop=(cch == 1)),
                            e.matmul(ps[bZ[g]][:, col0_:col0_ + nq_], lhsT=onesb[0:nk, :], rhs=pTt, start=(cch == 0), stop=(cch == 1)))[-1],
                           [pk, ("c", "onesb")] + vkeys, PSK(bU[g]) + PSK(bZ[g]))
                vb0 = c.cv["bin"] + c.C_V // 128
                for g, d in enumerate(DILS):
                    head = g * HPG + hi
                    zs, aa = 3 + g, 6 + g
                    nat = (lambda ap, d=d: ap if d == 1 else ap.rearrange("p (j r) -> p j r", r=d))
                    prm = (lambda ap, d=d: ap if d == 1 else ap.rearrange("p (r j) -> p j r", r=d))
                    act(lambda e, g=g, zs=zs, nat=nat, prm=prm: e.activation(out=nat(tmp[:, zs, :]), in_=prm(ps[bZ[g]][:, :]), func=AF.Identity),
                        PSK(bZ[g]), tk(zs))
                    dve(lambda e, g=g, zs=zs, aa=aa, head=head, nat=nat, prm=prm: e.scalar_tensor_tensor(out=nat(tmp[:, aa, :]), in0=nat(tmp[:, zs, :]),
                                                                                          scalar=cvec[:, vb0 + head:vb0 + head + 1], in1=prm(ps[bU[g]][:, :]), op0=ALU.mult, op1=ALU.add),
                        tk(zs) + PSK(bU[g]) + CALL, tk(aa))
                dve(lambda e: e.tensor_tensor(out=tmp[:, 6, :], in0=tmp[:, 6, :], in1=tmp[:, 7, :], op=ALU.add), tk(6) + tk(7), tk(6))
                dve(lambda e: e.tensor_tensor(out=tmp[:, 6, :], in0=tmp[:, 6, :], in1=tmp[:, 8, :], op=ALU.add), tk(6) + tk(8), tk(6))
                dve(lambda e: e.tensor_tensor(out=tmp[:, 3, :], in0=tmp[:, 3, :], in1=tmp[:, 4, :], op=ALU.add), tk(3) + tk(4), tk(3))
                dve(lambda e: e.tensor_tensor(out=tmp[:, 3, :], in0=tmp[:, 3, :], in1=tmp[:, 5, :], op=ALU.add), tk(3) + tk(5), tk(3))
                dve(lambda e: e.reciprocal(out=tmp[:, 4, :], in_=tmp[:, 3, :]), tk(3), tk(4))
                dve(lambda e, hi=hi: e.tensor_tensor(out=attnT[:, hi, :], in0=tmp[:, 6, :], in1=tmp[:, 4, :], op=ALU.mult), tk(6) + tk(4), YK(hi, hi + 1))
            dve(lambda e: e.memset(tmp[0:1, 0, 0:1], 0.0), [], tk(0) + tk(1) + tk(2) + akeys)
            stg(11)
            dw0 = c.cv["dw"]
            for ch in range(NCC):
                hb = ch % 2
                lo = pos0 - 15
                tl = sorted(set([lo // T, pos0 // T, (pos0 + T + 14) // T]))
                dma("pool", hw[:, hb, 0:T + 30], hg[ch * 128:(ch + 1) * 128, lo:lo + T + 30], [("hg", t_, ch // 4) for t_ in tl], [("hw", hb)])
                ck = YK(8 + 2 * ch, 10 + 2 * ch)
                dve(lambda e, ch=ch, hb=hb: e.tensor_scalar(cvf[:, ch, :], hw[:, hb, 0:T], cvec[:, dw0 + ch:dw0 + ch + 1], cvc("dwb", ch), ALU.mult, ALU.add),
                    [("hw", hb)] + CALL, ck)
                for w in range(1, CONVW):
                    dve(lambda e, ch=ch, hb=hb, w=w: e.scalar_tensor_tensor(out=cvf[:, ch, :], in0=hw[:, hb, w:w + T], scalar=cvec[:, dw0 + w * NCC + ch:dw0 + w * NCC + ch + 1],
                                                                            in1=cvf[:, ch, :], op0=ALU.mult, op1=ALU.add), [("hw", hb)] + ck + CALL, ck)
            layer_norm(lambda kc: cvf[:, kc, :], lambda kc: YK(8 + 2 * kc, 10 + 2 * kc), NCC, c.DCONV,
                       lambda kc: cvc("cg", kc), lambda kc: cvc("cb", kc), None, None, None,
                       lambda kc: (cvs[:, kc, :], YK(40 + kc, 41 + kc)), AF.Silu)
            stg(12)
            gt0 = c.cv["bin"] + c.C_GATE // 128
            cvrhs = (lambda kc: cvs[:, kc, :])
            cvrk = (lambda kc: YK(40 + kc, 41 + kc))
            arhs = (lambda kc: attnT[:, kc, :])
            ark = (lambda kc: YK(kc, kc + 1))

            def fin_a(ci, cc, b2, tslot):
                dve(lambda e: e.tensor_tensor(out=tmp[:, tslot, :], in0=ps[b2][:, :], in1=tmp[:, tslot, :], op=ALU.mult), PSK(b2) + tk(tslot), tk(tslot))
            for cg in range(0, D, 512):
                W = min(512, D - cg)
                tset[0] = 0
                for _ in gated("w_in", KC, c.C_GATE + cg, xrhs, xrk, AF.Sigmoid, (lambda ci, cg=cg: cvec[:, gt0 + cg // 128 + ci:gt0 + cg // 128 + ci + 1]),
                               "w_conv_out", NCC, cg, cvrhs, cvrk, W, fin_a):
                    pass

                def fin_b(ci, cc, b2, tslot, cg=cg):
                    m = cg // 128 + ci
                    dve(lambda e: e.tensor_tensor(out=tmp[:, tslot, :], in0=ps[b2][:, :], in1=tmp[:, tslot, :], op=ALU.mult), PSK(b2) + tk(tslot), tk(tslot))
                    dve(lambda e: e.tensor_tensor(out=mT[:, m, :], in0=tmp[:, tslot, :], in1=tmp[:, tslot - 4, :], op=ALU.add), tk(tslot) + tk(tslot - 4), hk(m))
                for _ in gated("w_in", KC, c.C_GATE + D + cg, xrhs, xrk, AF.Sigmoid, (lambda ci, cg=cg: cvec[:, gt0 + KC + cg // 128 + ci:gt0 + KC + cg // 128 + ci + 1]),
                               "w_attn_out", HPG, cg, arhs, ark, W, fin_b):
                    pass
            stg(13)
            for k0 in range(0, KC, 8):
                k1 = min(KC, k0 + 8)
                ks = []
                for kc in range(k0, k1):
                    ks += yk(kc)
                dma("pool", yT[:, k0:k1, :], x1s[k0 * 128:k1 * 128, pos0:pos0 + T].rearrange("(k p) t -> p k t", p=128), [("x1s", pos0, k0)], ks)

            def evac_o(m, b):
                dve(lambda e: e.tensor_tensor(out=yT[:, m, :], in0=ps[b][:, :], in1=yT[:, m, :], op=ALU.add), PSK(b) + yk(m), yk(m))
            proj_fm("w_out", 0, KC, 0, D, (lambda kc: mT[:, kc, :]), hk, evac_o)
            stg(14)
            main_ln(1)
            stg(15)
            ffn("w_ff2_gate", "w_ff2_up", "w_ff2_down")
            main_ln(2)
            stg(16)
            psrc = pa if inA else pb
            prow = pos0 - HALO if inA else pos0 - c.B0
            pst = tmp[:, 7:9, :].rearrange("p a t -> p (a t)")[:, 0:4 * c.DPLE].rearrange("p (m f) -> p m f", m=4)
            dma("act", pst, psrc[prow:prow + T, :].rearrange("(m p) f -> p m f", p=128), [], tk(7) + tk(8))
            for kp in range(KP):
                b = nb()
                pe(lambda e, b=b, kp=kp: [e.transpose(out=ps[b][:, m * 128:(m + 1) * 128], in_=pst[:, m, kp * 128:(kp + 1) * 128], identity=ident[:, :]) for m in range(4)][-1],
                   tk(7) + tk(8) + [("c", "ident")], PSK(b))
                dve(lambda e, b=b, kp=kp: e.tensor_copy(out=pT[:, kp, :], in_=ps[b][:, :]), PSK(b), [("pT_", kp)])
            bp0 = c.cv["bpg"]

            def fin_p(ci, cc, b2, tslot):
                dve(lambda e: e.tensor_tensor(out=tmp[:, tslot, :], in0=ps[b2][:, :], in1=tmp[:, tslot, :], op=ALU.mult), PSK(b2) + tk(tslot), tk(tslot))
                dve(lambda e: e.tensor_tensor(out=yT[:, ci, :], in0=yT[:, ci, :], in1=tmp[:, tslot, :], op=ALU.add), yk(ci) + tk(tslot), yk(ci))
            tset[0] = 0
            for _ in gated("w_ple_gate", KC, 0, xrhs, xrk, AF.Sigmoid, (lambda ci: cvec[:, bp0 + ci:bp0 + ci + 1]),
                           "w_ple", KP, 0, (lambda kp: pT[:, kp, :]), (lambda kp: [("pT_", kp)]), D, fin_p):
                tset[0] = 0
            main_ln(3, want_x=False)
            stg(17)
            ydst = ya if inA else yb
            for m in range(4):
                sbi = m % 2
                for kq in range(KC // GK):
                    b = nb()
                    rk = []
                    for i in range(GK):
                        rk += yk(kq * GK + i)
                    pe(lambda e, b=b, kq=kq, m=m: [e.transpose(out=ps[b][:, i * 128:(i + 1) * 128], in_=yT[:, kq * GK + i, m * 128:(m + 1) * 128], identity=ident[:, :]) for i in range(GK)][-1],
                       rk + [("c", "ident")], PSK(b))
                    o_ap = stage[sbi][:, kq * GK * 128:(kq + 1) * GK * 128]
                    npg = max(1, GK // 2)
                    okeys = [("hp", sbi * SP + kq * npg + i_) for i_ in range(npg)]
                    if kq % 2 == 0:
                        act(lambda e, b=b, o_ap=o_ap: e.activation(out=o_ap, in_=ps[b][:, 0:GK * 128], func=AF.Identity), PSK(b), okeys)
                    else:
                        dve(lambda e, b=b, o_ap=o_ap: e.tensor_copy(out=o_ap, in_=ps[b][:, 0:GK * 128]), PSK(b), okeys)
                dma("act", ydst[prow + m * 128:prow + (m + 1) * 128, :], stage[sbi][:, :], stagek[sbi], [("out", pos0, m)])

    try:
        body()
    except _Stop:
        pass
    emit(nc, P, es)
    return nc


def emit(nc, P, es):
    ops = P.ops
    n = len(ops)
    sig = [False] * n
    for i, (eng, fn, deps, dma) in enumerate(ops):
        for d in deps:
            de, _, _, ddma = ops[d]
            if ddma:
                continue
            if (not dma) and de == eng and (eng == "pe" or not SAME_ENG_SYNC):
                continue
            sig[d] = True
    csem = {e: es.enter_context(nc.semaphore("c_" + e)) for e in ("pe", "act", "dve")}
    dsem = {q: [es.enter_context(nc.semaphore("d_%s%d" % (q, i))) for i in range(KR)] for q in ("sp", "act", "pool")}
    cnt = {"pe": 0, "act": 0, "dve": 0}
    dcnt = {"sp": 0, "act": 0, "pool": 0}
    semof = [None] * n
    streams = {e: [] for e in ("pe", "act", "dve", "pool", "sp")}
    for i, (eng, fn, deps, dma) in enumerate(ops):
        streams[eng].append(i)
        if dma:
            k = dcnt[eng]
            dcnt[eng] += 1
            semof[i] = (dsem[eng][k % KR], 16 * (k // KR + 1), k)
        elif sig[i]:
            cnt[eng] += 1
            semof[i] = (csem[eng], cnt[eng], -1)
    final = []
    for q in dsem:
        for r_ in range(KR):
            tot = (dcnt[q] - r_ + KR - 1) // KR if dcnt[q] > r_ else 0
            if tot > 0:
                final.append((dsem[q][r_], 16 * tot))

    def run(engname, e):
        waited = {}
        for i in streams[engname]:
            eng, fn, deps, dma = ops[i]
            need = {}
            for d in deps:
                de, _, _, ddma = ops[d]
                if (not ddma) and (not dma) and de == eng and (eng == "pe" or not SAME_ENG_SYNC):
                    continue
                s, v, _k = semof[d]
                key = id(s)
                if need.get(key, (None, 0))[1] < v:
                    need[key] = (s, v)
            if dma:
                s, v, k = semof[i]
                if k >= KR:
                    pv = 16 * (k // KR)
                    key = id(s)
                    if need.get(key, (None, 0))[1] < pv:
                        need[key] = (s, pv)
            for key, (s, v) in need.items():
                if waited.get(key, 0) < v:
                    e.wait_ge(s, v)
                    waited[key] = v
            ins = fn(e)
            if semof[i] is not None:
                ins.then_inc(semof[i][0], 16 if dma else 1)
        if engname == "sp":
            for s, v in final:
                e.wait_ge(s, v)

    with nc.Block() as block:
        @block.tensor
        def _(e):
            run("pe", e)

        @block.scalar
        def _(e):
            run("act", e)

        @block.vector
        def _(e):
            run("dve", e)

        @block.gpsimd
        def _(e):
            run("pool", e)

        @block.sync
        def _(e):
            run("sp", e)
    es.close()


def rel_bucket(rel):
    half = 16
    max_exact = 8
    ret = (rel > 0).astype(np.int32) * half
    n = np.abs(rel)
    large = max_exact + (np.log(np.maximum(n, 1) / max_exact) / np.log(1024 / max_exact) * (half - max_exact)).astype(np.int32)
    large = np.minimum(large, half - 1)
    return (ret + np.where(n < max_exact, n, large)).astype(np.int32)


def fm(v):
    return np.ascontiguousarray(np.asarray(v, np.float32).reshape(-1, 128).T)


def host_prep(cfg, inp):
    c = cfg
    f = lambda k: np.asarray(inp[k], np.float32)
    shared = {}
    for k in ("w_ff1_gate", "w_ff1_up", "w_ff1_down", "w_in", "w_conv_out", "w_attn_out", "w_out", "w_ff2_gate", "w_ff2_up", "w_ff2_down", "w_ple", "w_ple_gate"):
        shared[k] = np.ascontiguousarray(f(k)[0])
    ln_g, ln_b = f("ln_g")[0], f("ln_b")[0]
    cols = [fm(ln_g[i]) for i in range(4)] + [fm(ln_b[i]) for i in range(4)] + [fm(f("b_in")[0])]
    dw = f("conv_dw")[0]
    cols += [fm(dw[w]) for w in range(CONVW)]
    cols += [fm(f("conv_dw_b")[0]), fm(f("conv_ln_g")[0]), fm(f("conv_ln_b")[0]), fm(f("b_ple_gate")[0])]
    cvec = np.ascontiguousarray(np.concatenate(cols, axis=1))
    assert cvec.shape == (128, c.NCV), cvec.shape
    shared["cvec"] = cvec
    shared["ident"] = np.eye(128, dtype=np.float32)
    rb = f("rel_bias")
    bT = np.empty((c.NH, 128, 256), np.float32)
    ii = np.arange(128)[:, None]
    jj = np.arange(128)[None, :]
    for g, d in enumerate(DILS):
        for cch in range(2):
            rel = cch * 128 + ii - 64 - jj
            bk = rel_bucket(rel * d)
            ok = np.abs(rel) <= 64
            for hi in range(c.HPG):
                h = g * c.HPG + hi
                bT[h, :, cch * 128:(cch + 1) * 128] = np.where(ok, rb[bk, h], np.float32(NEG))
    shared["biasT"] = bT
    xp, xs_, pp, psm = f("x_prompt"), f("x_sample"), f("p_prompt")[0], f("p_sample")[0]
    maps = []
    for core in range(8):
        b, qtr = core // 4, core % 4
        g0 = qtr * c.QP - HALO
        xa = np.zeros((c.LA, c.D), np.float32)
        lo, hi_ = max(0, g0), min(c.SEQ, g0 + c.LA)
        xa[lo - g0:hi_ - g0] = xp[b, lo:hi_]
        valid = np.zeros(c.NPOS, bool)
        valid[lo - g0:hi_ - g0] = True
        valid[c.B0:c.B0 + c.QS] = True
        pmask = np.stack([np.broadcast_to(valid[p:p + T].astype(np.float32), (128, T)) for p in c.t1]).copy()
        kmask = np.zeros((len(c.t2), 128, c.NMK), np.float32)
        for ti, pos0 in enumerate(c.t2):
            for g, d in enumerate(DILS):
                nsub = T // d
                nwc = -(-(nsub + 128) // 128)
                for wc in range(nwc):
                    for r in range(d):
                        pos = pos0 - 64 * d + (wc * 128 + np.arange(128)) * d + r
                        ok = (pos < c.NPOS) & valid[np.minimum(pos, c.NPOS - 1)]
                        kmask[ti, :, c.moff[g] + wc * d + r] = np.where(ok, 0.0, NEG)
        m = dict(shared)
        m.update(xa=xa, xb=np.ascontiguousarray(xs_[core]), pa=np.ascontiguousarray(pp[b, qtr * c.QP:(qtr + 1) * c.QP]),
                 pb=np.ascontiguousarray(psm[core]), pmask=pmask, kmask=kmask)
        maps.append(m)
    return maps


def run_cfg(cfg, inp, limit=99):
    nc = build(cfg, limit)
    maps = host_prep(cfg, inp)
    res = run_bass_kernel_spmd(nc, maps, core_ids=list(range(8)))
    yp = np.empty((2, cfg.SEQ, cfg.D), np.float32)
    ysm = np.empty((8, cfg.QS, cfg.D), np.float32)
    for core in range(8):
        b, qtr = core // 4, core % 4
        yp[b, qtr * cfg.QP:(qtr + 1) * cfg.QP] = res.results[core]["ya"]
        ysm[core] = res.results[core]["yb"]
    return yp, ysm


def kernel(**inputs):
    return run_cfg(Cfg(), inputs)
```
